# Optimizing a Trainium2 kernel written in Bass

```python
import math
import jax, jax.numpy as jnp
from jax import lax
import numpy as np

D_MODEL = 2048
BATCH = 4
SEQ = 4096
DEPTH = 1

CHUNK = 64
PLE_DIM = 256
EPS = 1e-6
A_HEADS = 8
A_HEAD_DIM = 128
A_WIDTH = A_HEADS * A_HEAD_DIM
IDX_HEADS = 16
IDX_DIM = 64
TOPK_MAX = 256
R_HEADS = 8
R_KEY_DIM = 128
R_VAL_DIM = 128
R_WIDTH = R_HEADS * R_VAL_DIM
ROPE_BASE = 10000.0
N_BUCKETS = 32
MAX_DISTANCE = 128

SPLITS = (A_WIDTH, A_WIDTH, A_WIDTH, A_WIDTH,
          IDX_HEADS * IDX_DIM, IDX_DIM, IDX_HEADS,
          R_HEADS * R_KEY_DIM, R_HEADS * R_KEY_DIM,
          R_WIDTH, R_WIDTH,
          D_MODEL, D_MODEL)
IN_WIDTH = int(sum(SPLITS))
SPLIT_POINTS = tuple(int(s) for s in np.cumsum(SPLITS)[:-1])

kernel_name = 'hybrid_dsa_retention_gated_block'


def rmsnorm(x, g):
    xf = x.astype(jnp.float32)
    y = xf * lax.rsqrt(jnp.mean(xf * xf, axis=-1, keepdims=True) + EPS) * g.astype(jnp.float32)
    return y.astype(x.dtype)


def plain_layernorm(x):
    xf = x.astype(jnp.float32)
    mu = jnp.mean(xf, axis=-1, keepdims=True)
    var = jnp.mean((xf - mu) ** 2, axis=-1, keepdims=True)
    return ((xf - mu) * lax.rsqrt(var + EPS)).astype(x.dtype)


def rope(x, positions):
    half = x.shape[-1] // 2
    inv_freq = ROPE_BASE ** (-jnp.arange(half, dtype=jnp.float32) / half)
    ang = positions[:, :, None].astype(jnp.float32) * inv_freq
    cos = jnp.cos(ang)[:, :, None, :]
    sin = jnp.sin(ang)[:, :, None, :]
    xf = x.astype(jnp.float32)
    x1, x2 = xf[..., :half], xf[..., half:]
    return jnp.concatenate([x1 * cos - x2 * sin, x1 * sin + x2 * cos], axis=-1).astype(x.dtype)


def t5_bucket(rel):
    half = N_BUCKETS // 2
    max_exact = half // 2
    ret = jnp.where(rel > 0, half, 0)
    n = jnp.abs(rel)
    nf = jnp.maximum(n, 1).astype(jnp.float32)
    large = max_exact + (jnp.log(nf / max_exact) / math.log(MAX_DISTANCE / max_exact)
                         * (half - max_exact)).astype(jnp.int32)
    large = jnp.minimum(large, half - 1)
    return ret + jnp.where(n < max_exact, n, large)


def dsa_mixer(q, k, v, qi, ki, w, positions, rel_table):
    B, L = q.shape[0], q.shape[1]
    nc = L // CHUNK
    topk = min(TOPK_MAX, L // 4)
    scale = A_HEAD_DIM ** -0.5
    gather = jax.vmap(lambda t, i: t[i])

    def blocks(t):
        return t.reshape((B, nc, CHUNK) + t.shape[2:]).swapaxes(0, 1)

    def one_chunk(args):
        c, q_c, qi_c, w_c, pos_c = args
        limit = (c + 1) * CHUNK
        score = jnp.einsum('bqhd,bsd->bqsh', qi_c, ki)
        index = jnp.einsum('bqsh,bqh->bqs', jax.nn.relu(score), w_c).astype(jnp.float32)
        admissible = jnp.arange(L) < limit
        index = jnp.where(admissible[None, None, :], index, -jnp.inf)
        _, sel = lax.top_k(index, topk)
        sel_ok = sel < limit
        k_sel = gather(k, sel)
        v_sel = gather(v, sel)
        pos_sel = gather(positions, sel)
        bias = rel_table.astype(jnp.float32)[t5_bucket(pos_sel - pos_c[:, :, None])]
        logits = jnp.einsum('bqhd,bqkhd->bqkh', q_c, k_sel).astype(jnp.float32) * scale + bias
        logits = jnp.where(sel_ok[..., None], logits, -jnp.inf)
        probs = jax.nn.softmax(logits, axis=2).astype(v.dtype)
        return jnp.einsum('bqkh,bqkhd->bqhd', probs, v_sel)

    out = lax.map(one_chunk, (jnp.arange(nc), blocks(q), blocks(qi), blocks(w), blocks(positions)))
    return out.swapaxes(0, 1).reshape(B, L, A_WIDTH)


def retention_mixer(q, k, v, gn_gain):
    B, L = q.shape[0], q.shape[1]
    nc = L // CHUNK
    dt = v.dtype
    qf, kf, vf = q.astype(jnp.float32), k.astype(jnp.float32), v.astype(jnp.float32)
    log_g = jnp.log(1.0 - 2.0 ** (-5.0 - jnp.arange(R_HEADS, dtype=jnp.float32)))
    pos = jnp.arange(CHUNK, dtype=jnp.float32)
    dist = jnp.abs(pos[:, None] - pos[None, :])
    intra_decay = jnp.exp(log_g[:, None, None] * dist)
    to_end = jnp.exp(log_g[None, :] * (CHUNK - 1.0 - pos)[:, None])
    from_start = jnp.exp(log_g[None, :] * (pos + 1.0)[:, None])
    chunk_decay = jnp.exp(log_g * CHUNK)
    qc = qf.reshape(B, nc, CHUNK, R_HEADS, R_KEY_DIM)
    kc = kf.reshape(B, nc, CHUNK, R_HEADS, R_KEY_DIM)
    vc = vf.reshape(B, nc, CHUNK, R_HEADS, R_VAL_DIM)
    scores = jnp.einsum('bnqhd,bnkhd->bnhqk', qc, kc) * intra_decay
    intra = jnp.einsum('bnhqk,bnkhe->bnqhe', scores, vc)
    kv = jnp.einsum('bnkhd,bnkhe->nbhde', kc * to_end[:, :, None], vc)

    def step(state, kv_c):
        return state * chunk_decay[None, :, None, None] + kv_c, state

    _, prev = lax.scan(step, jnp.zeros(kv.shape[1:], jnp.float32), kv)
    cross = jnp.einsum('bnqhd,nbhde->bnqhe', qc * from_start[:, :, None], prev)
    y = (intra + cross).reshape(B, L, R_HEADS, R_VAL_DIM)
    mu = jnp.mean(y, axis=-1, keepdims=True)
    var = jnp.mean((y - mu) ** 2, axis=-1, keepdims=True)
    y = ((y - mu) * lax.rsqrt(var + EPS)).reshape(B, L, R_WIDTH) * gn_gain.astype(jnp.float32)
    return y.astype(dt)


def hybrid_layer(x, p_i, positions, w_in, norm_gain, w_a_out, w_b_out, w_o, ret_gn_gain,
                 w_ple, w_ple_gate, rel_bias):
    B, L = x.shape[0], x.shape[1]
    h = rmsnorm(x, norm_gain)
    proj = h @ w_in
    (aq, ak, av, az, iq, ik, iw, rq, rk, rv, rz, ga, gb) = jnp.split(proj, SPLIT_POINTS, axis=-1)
    aq = aq.reshape(B, L, A_HEADS, A_HEAD_DIM)
    ak = ak.reshape(B, L, A_HEADS, A_HEAD_DIM)
    av = av.reshape(B, L, A_HEADS, A_HEAD_DIM)
    iq = iq.reshape(B, L, IDX_HEADS, IDX_DIM) * (IDX_DIM ** -0.5)
    ik = plain_layernorm(ik)
    iw = iw * (IDX_HEADS ** -0.5)
    a_out = dsa_mixer(aq, ak, av, iq, ik, iw, positions, rel_bias) * jax.nn.silu(az)
    rq = rope(rq.reshape(B, L, R_HEADS, R_KEY_DIM), positions)
    rk = rope(rk.reshape(B, L, R_HEADS, R_KEY_DIM), positions) * (R_KEY_DIM ** -0.5)
    rv = rv.reshape(B, L, R_HEADS, R_VAL_DIM)
    b_out = retention_mixer(rq, rk, rv, ret_gn_gain) * jax.nn.silu(rz)
    merged = jax.nn.sigmoid(ga) * (a_out @ w_a_out) + jax.nn.sigmoid(gb) * (b_out @ w_b_out)
    r = x + merged @ w_o
    return r + (p_i @ w_ple) * jax.nn.sigmoid(r @ w_ple_gate)


def setup_inputs(seed: int = 0) -> dict:
    key = jax.random.key(seed)
    ks = jax.random.split(key, 14)
    f32 = jnp.float32
    x = jax.random.normal(ks[0], (BATCH, SEQ, D_MODEL), f32)
    p = jax.random.normal(ks[1], (DEPTH, BATCH, SEQ, PLE_DIM), f32)
    offsets = jax.random.randint(ks[2], (BATCH, 1), 0, 64) * CHUNK
    positions = (offsets + jnp.arange(SEQ)[None, :]).astype(jnp.int32)
    w_in = jax.random.normal(ks[3], (DEPTH, D_MODEL, IN_WIDTH), f32) * D_MODEL ** -0.5
    norm_gain = 1.0 + 0.02 * jax.random.normal(ks[4], (DEPTH, D_MODEL), f32)
    w_a_out = jax.random.normal(ks[5], (DEPTH, A_WIDTH, D_MODEL), f32) * A_WIDTH ** -0.5
    w_b_out = jax.random.normal(ks[6], (DEPTH, R_WIDTH, D_MODEL), f32) * R_WIDTH ** -0.5
    w_o = jax.random.normal(ks[7], (DEPTH, D_MODEL, D_MODEL), f32) * D_MODEL ** -0.5
    ret_gn_gain = 1.0 + 0.02 * jax.random.normal(ks[8], (DEPTH, R_WIDTH), f32)
    w_ple = jax.random.normal(ks[9], (DEPTH, PLE_DIM, D_MODEL), f32) * PLE_DIM ** -0.5
    w_ple_gate = jax.random.normal(ks[10], (DEPTH, D_MODEL, D_MODEL), f32) * D_MODEL ** -0.5
    rel_bias = 0.5 * jax.random.normal(ks[11], (N_BUCKETS, A_HEADS), f32)
    final_gain = 1.0 + 0.02 * jax.random.normal(ks[12], (D_MODEL,), f32)
    return {'x': x, 'p': p, 'positions': positions, 'w_in': w_in, 'norm_gain': norm_gain,
            'w_a_out': w_a_out, 'w_b_out': w_b_out, 'w_o': w_o, 'ret_gn_gain': ret_gn_gain,
            'w_ple': w_ple, 'w_ple_gate': w_ple_gate, 'rel_bias': rel_bias,
            'final_gain': final_gain}


def reference(x, p, positions, w_in, norm_gain, w_a_out, w_b_out, w_o, ret_gn_gain,
              w_ple, w_ple_gate, rel_bias, final_gain):
    for i in range(DEPTH):
        x = hybrid_layer(x, p[i], positions, w_in[i], norm_gain[i], w_a_out[i], w_b_out[i],
                         w_o[i], ret_gn_gain[i], w_ple[i], w_ple_gate[i], rel_bias)
    return rmsnorm(x, final_gain)
```

```python
import math
from contextlib import ExitStack

import numpy as np
import ml_dtypes

import concourse.bass as bass
import concourse.mybir as mybir
from concourse.bass_utils import run_bass_kernel_spmd

F32 = mybir.dt.float32
BF16 = mybir.dt.bfloat16
I32 = mybir.dt.int32
AF = mybir.ActivationFunctionType
ALU = mybir.AluOpType

import os
MAXOPS = int(os.environ.get("MAXOPS", "100000000"))
ENGS = ["pe", "act", "pool", "dve", "sp"]
BLK = {"pe": "tensor", "act": "scalar", "pool": "gpsimd", "dve": "vector", "sp": "sync"}


class Prog:
    def __init__(self, nc, es):
        self.nc, self.es = nc, es
        self.ops = []
        self.W, self.R = {}, {}
        self.last_eng = {}
        self.last_dma = {}

    def add(self, eng, fn, r=(), w=(), pw=(), dma=None):
        i = len(self.ops)
        if i >= MAXOPS and fn is not None:
            return i
        deps = set()
        for x in r:
            deps.update(self.W.get(x, ()))
        for x in w:
            deps.update(self.W.get(x, ()))
            deps.update(self.R.get(x, ()))
        for x in pw:
            deps.update(self.R.get(x, ()))
        for x in r:
            self.R.setdefault(x, []).append(i)
        for x in w:
            self.W[x] = [i]
            self.R[x] = []
        for x in pw:
            if self.R.get(x):
                self.W[x] = [i]
                self.R[x] = []
            else:
                self.W.setdefault(x, []).append(i)
        self.ops.append(dict(eng=eng, fn=fn, deps=deps, dma=dma))
        if dma is None:
            self.last_eng[eng] = i
        else:
            self.last_dma[dma] = i
        return i

    def barrier(self):
        deps = set(self.last_eng.values()) | set(self.last_dma.values())
        for n, e in enumerate(ENGS):
            i = len(self.ops)
            self.ops.append(dict(eng=e, fn=None, deps=set(deps), dma=None, bar=(n == 0)))
        self.W, self.R = {}, {}

    def emit(self):
        nc, ops = self.nc, self.ops

        def skip(od, o):
            return od["dma"] is None and o["dma"] is None and od["eng"] == "pe" and o["eng"] == "pe"

        signaled = [False] * len(ops)
        for o in ops:
            for d in o["deps"]:
                if not skip(ops[d], o):
                    signaled[d] = True
        sems = {}

        def sem(name):
            if name not in sems:
                sems[name] = self.es.enter_context(nc.semaphore("s_" + name))
            return sems[name]

        cnt = {}
        cur, free, nd = {}, [], [0]
        for i, o in enumerate(ops):
            if o.get("bar"):
                free.extend(cur.values())
                cur = {}
            if o["fn"] is None:
                o["sig"] = None
            elif o["dma"] is not None:
                if o["dma"] not in cur:
                    if free:
                        cur[o["dma"]] = free.pop()
                    else:
                        cur[o["dma"]] = "d_%d" % nd[0]
                        nd[0] += 1
                k = cur[o["dma"]]
                cnt[k] = cnt.get(k, 0) + 16
                o["sig"] = (k, cnt[k])
            elif signaled[i]:
                k = "e_" + o["eng"]
                cnt[k] = cnt.get(k, 0) + 1
                o["sig"] = (k, cnt[k])
            else:
                o["sig"] = None
        self.nsems = len(cnt)
        for k in cnt:
            sem(k)
        block = self.es.enter_context(nc.Block())
        for e in ENGS:
            mine = [o for o in ops if o["eng"] == e]

            def body(eng, mine=mine):
                waited = {}
                for o in mine:
                    need = {}
                    for d in o["deps"]:
                        od = ops[d]
                        if od["sig"] is None or skip(od, o):
                            continue
                        k, v = od["sig"]
                        if need.get(k, 0) < v:
                            need[k] = v
                    for k, v in need.items():
                        if waited.get(k, 0) < v:
                            eng.wait_ge(sems[k], v)
                            waited[k] = v
                    if o["fn"] is None:
                        continue
                    ins = o["fn"](eng)
                    if o["sig"] is not None:
                        ins.then_inc(sems[o["sig"][0]], 16 if o["dma"] is not None else 1)

            getattr(block, BLK[e])(body)


D = 2048
KC = 16
NOWN = 2048
NVIRT = 4096
NEG = -1.0e30
C_AQ, C_AK, C_AV, C_AZ, C_IQ, C_IK, C_IW, C_RQ, C_RK, C_RV, C_RZ, C_GA, C_GB, C_END = (
    0, 1024, 2048, 3072, 4096, 5120, 5184, 5200, 6224, 7248, 8272, 9296, 11344, 13392)
EPS = 1e-6
T5_THR = [1, 2, 3, 4, 5, 6, 7, 8, 12, 16, 23, 32, 46, 64, 91]
CF_HMA, CF_HMM, CF_TE, CF_IF, CF_FS, CF_M, CF_N = 0, 1, 2, 10, 74, 74 + 1024, 74 + 2048


def _t5_bucket(rel):
    n = abs(rel)
    b = sum(1 for t in T5_THR if n >= t)
    return b + (16 if rel > 0 else 0)


def host_consts():
    cf = np.zeros((128, CF_N), np.float32)
    p = np.arange(128)
    cf[:, CF_HMA] = np.where(p < 64, NEG, 0.0)
    cf[:, CF_HMM] = np.where(p < 64, 0.0, 1.0)
    g = 1.0 - 2.0 ** (-5.0 - np.arange(8, dtype=np.float64))
    for h in range(8):
        cf[:, CF_TE + h] = g[h] ** (127 - p)
        cf[:, CF_FS + h * 128:CF_FS + (h + 1) * 128] = (g[h] ** (p + 1.0))[None, :]
        k = p[:, None]
        q = p[None, :]
        same = (k // 64) == (q // 64)
        m = np.where(same, g[h] ** np.abs(q - k), np.where(k < q, g[h] ** np.maximum(q - k, 0), 0.0))
        cf[:, CF_M + h * 128:CF_M + (h + 1) * 128] = m
    half = 64
    cf[:, CF_IF:CF_IF + 64] = (10000.0 ** (-np.arange(half, dtype=np.float32) / half)).astype(np.float32)[None, :]
    o1 = np.zeros((32, 383), np.float32)
    for m_ in range(383):
        o1[_t5_bucket(127 - m_), m_] = 1.0
    g128 = [float(g[h] ** 128) for h in range(8)]
    return cf, o1, g128


class Arena:
    def __init__(self, nc, nbytes):
        self.nc = nc
        t = nc.alloc_sbuf_tensor("arena", [128, nbytes // 4], F32)
        self.base = nc.lookup_mloc(t).addr
        self.size = nbytes
        self.off = 0
        self.n = 0

    def mark(self):
        return self.off

    def reset(self, m):
        self.off = m

    def t(self, name, shape, dt):
        nb = int(np.prod(shape[1:])) * (4 if dt in (F32, I32) else 2)
        nb = (nb + 63) // 64 * 64
        assert self.off + nb <= self.size, (name, self.off, nb, self.size)
        self.n += 1
        h = self.nc.alloc_sbuf_tensor_at("%s_%d" % (name, self.n), shape, dt, offset=self.base + self.off)
        self.off += nb
        return h


def build(dbg=False, stop_after=99, phases=None, vlist=None):
    nc = bass.Bass("TRN2", target_bir_lowering=False)
    es = ExitStack()
    P = Prog(nc, es)
    cf_np, o1_np, g128 = host_consts()

    def din(name, shape, dt=F32):
        return nc.dram_tensor(name, shape, dt, kind="ExternalInput")

    xv = din("xv", [NVIRT, D]).ap()
    p_o = din("p_o", [NOWN, 256]).ap()
    posv = din("posv", [NVIRT, 1], I32).ap()
    posT_d = din("posT", [128, 32], I32).ap()
    w_in = din("w_in", [D, C_END]).ap()
    gainT_d = din("gainT", [128, KC]).ap()
    w_a = din("w_a_out", [1024, D]).ap()
    w_b = din("w_b_out", [1024, D]).ap()
    w_o = din("w_o", [D, D]).ap()
    w_g = din("w_ple_gate", [D, D]).ap()
    w_p = din("w_ple", [256, D]).ap()
    gn_d = din("ret_gn_gain", [1, 1024]).ap()
    relb_d = din("rel_bias", [32, 8]).ap()
    relrow_d = din("rel_row", [1, 256]).ap()
    fg_d = din("final_gain", [1, D]).ap()
    cmask_d = din("cmask", [128, 2]).ap()
    cf_d = din("cf", [128, CF_N]).ap()
    o1_d = din("o1", [32, 383]).ap()
    out_d = nc.dram_tensor("out", [NOWN, D], F32, kind="ExternalOutput").ap()

    sk = "ExternalOutput" if dbg else "Internal"

    def scr(name, shape, dt):
        return nc.dram_tensor(name, shape, dt, kind=sk)

    S_akT = scr("S_akT", [8, 128, NVIRT], BF16).ap()
    S_aqT = scr("S_aqT", [8, 128, NOWN], BF16).ap()
    S_iqT = scr("S_iqT", [8, 128, NOWN], BF16).ap()
    S_av = scr("S_av", [NVIRT, 1024], BF16).ap()
    S_saz = scr("S_saz", [NOWN, 1024], BF16).ap()
    S_ikn = scr("S_ikn", [NVIRT, 64], BF16).ap()
    S_iw = scr("S_iw", [NOWN, 16], F32).ap()
    S_rq = scr("S_rq", [NOWN, 1024], F32).ap()
    S_rk = scr("S_rk", [NVIRT, 1024], F32).ap()
    S_rv = scr("S_rv", [NVIRT, 1024], BF16).ap()
    S_srz = scr("S_srz", [NOWN, 1024], BF16).ap()
    S_sga = scr("S_sga", [NOWN, D], BF16).ap()
    S_sgb = scr("S_sgb", [NOWN, D], BF16).ap()
    S_F_t = scr("S_F", [8, 383], F32)
    S_bo = scr("S_bo", [NOWN, 1024], BF16).ap()
    S_ao = scr("S_ao", [NOWN, 1024], BF16).ap()
    S_mT = scr("S_mT", [16, 128, 32 * 128], BF16).ap()
    S_mg = scr("S_mg", [NOWN, D], BF16).ap()
    S_r = scr("S_r", [NOWN, D], F32).ap()
    S_rb = scr("S_rb", [NOWN, D], BF16).ap()

    A = Arena(nc, 200 * 1024)
    banks = [nc.alloc_psum_tensor("bank%d" % i, [128, 512], F32) for i in range(8)]
    bankb = [b[:].bitcast(BF16) for b in banks]

    def dma(out, in_, key, r=(), w=(), pw=()):
        return P.add("sp", lambda e: e.dma_start(out=out, in_=in_), r=r, w=w, pw=pw, dma=key)

    def mm(out, lhsT, rhs, start, stop, r, w=(), pw=()):
        return P.add("pe", lambda e: e.matmul(out, lhsT=lhsT, rhs=rhs, start=start, stop=stop), r=r, w=w, pw=pw)

    def tr(out, in_, ident, r, w=(), pw=()):
        return P.add("pe", lambda e: e.transpose(out=out, in_=in_, identity=ident), r=r, w=w, pw=pw)

    def act(out, in_, func, r, w=(), pw=(), scale=1.0, bias=0.0):
        return P.add("act", lambda e: e.activation(out=out, in_=in_, func=func, scale=scale, bias=bias), r=r, w=w, pw=pw)

    def ts(eng, out, in0, s1, s2, op0, op1, r, w=(), pw=()):
        if op1 is None:
            return P.add(eng, lambda e: e.tensor_scalar(out=out, in0=in0, scalar1=s1, scalar2=None, op0=op0), r=r, w=w, pw=pw)
        return P.add(eng, lambda e: e.tensor_scalar(out=out, in0=in0, scalar1=s1, scalar2=s2, op0=op0, op1=op1), r=r, w=w, pw=pw)

    def tt(eng, out, in0, in1, op, r, w=(), pw=()):
        return P.add(eng, lambda e: e.tensor_tensor(out=out, in0=in0, in1=in1, op=op), r=r, w=w, pw=pw)

    def stt(out, in0, scalar, in1, op0, op1, r, w=(), pw=()):
        return P.add("dve", lambda e: e.scalar_tensor_tensor(out=out, in0=in0, scalar=scalar, in1=in1, op0=op0, op1=op1), r=r, w=w, pw=pw)

    def cp(eng, out, in_, r, w=(), pw=()):
        if eng == "act":
            return P.add("act", lambda e: e.activation(out=out, in_=in_, func=AF.Copy), r=r, w=w, pw=pw)
        return P.add(eng, lambda e: e.tensor_copy(out=out, in_=in_), r=r, w=w, pw=pw)

    def memset(eng, ap, val, w):
        return P.add(eng, lambda e: e.memset(ap, val), w=w)

    bank_rr = [0]

    def nextbank(lo=0, hi=8):
        b = lo + bank_rr[0] % (hi - lo)
        bank_rr[0] += 1
        return b

    ident_f = A.t("ident_f", [128, 128], F32)
    ident = A.t("ident", [128, 128], BF16)
    jf = A.t("jf", [128, 128], F32)
    cf = A.t("cf", [128, CF_N], F32)
    cmask = A.t("cmask", [128, 2], F32)
    gainT = A.t("gainT", [128, KC], F32)
    tab = A.t("tab", [128, 256], F32)
    negb15 = A.t("negb15", [128, 8], F32)
    E0 = A.t("E0", [128, 8, 128], F32)
    E1 = A.t("E1", [128, 8, 128], F32)
    m_const = A.mark()
    memset("pool", ident_f[:], 1.0, ["ident_f"])
    P.add("pool", lambda e: e.affine_select(out=ident_f[:], in_=ident_f[:], pattern=[[-1, 128]], compare_op=ALU.is_equal,
                                           fill=0.0, base=0, channel_multiplier=1), r=["ident_f"], w=["ident_f"])
    cp("dve", ident[:], ident_f[:], ["ident_f"], ["ident"])
    memset("pool", jf[:], 1.0, ["jf"])
    P.add("pool", lambda e: e.affine_select(out=jf[:], in_=jf[:], pattern=[[1, 128]], compare_op=ALU.is_equal,
                                           fill=0.0, base=-127, channel_multiplier=1), r=["jf"], w=["jf"])
    dma(cf[:], cf_d, "c1", w=["cf"])
    dma(cmask[:], cmask_d, "c2", w=["cmask"])
    dma(gainT[:], gainT_d, "c3", w=["gainT"])
    dma(tab[:], relrow_d.partition_broadcast(128), "c4", w=["tab"])
    ts("dve", negb15[:], tab[:, 120:128], -(128.0 ** 0.5), None, ALU.mult, None, ["tab"], ["negb15"])
    relb = A.t("relb", [32, 8], F32)
    o1 = A.t("o1", [32, 383], F32)
    Fsb = A.t("Fsb", [8, 383], F32)
    Hk = A.t("Hk", [128, 8, 128], F32)
    dma(relb[:], relb_d, "c5", w=["relb"])
    dma(o1[:], o1_d, "c6", w=["o1"])
    mm(banks[0][0:8, 0:383], relb[:], o1[:], True, True, ["relb", "o1"], w=["b0"])
    cp("dve", Fsb[:], banks[0][0:8, 0:383], ["b0"], ["Fsb"])
    dma(S_F_t.ap(), Fsb[:], "c7", r=["Fsb"], w=["S_F"])
    for c, Et, nm in ((0, E0, "E0"), (128, E1, "E1")):
        hank = bass.AP(S_F_t, c, [[1, 128], [383, 8], [1, 128]])
        dma(Hk[:], hank, "c8", r=["S_F"], w=["Hk"])
        for hb in range(2):
            mm(banks[1 + hb][:, :], jf[:], Hk[:, hb * 4:(hb + 1) * 4, :].rearrange("p a b -> p (a b)"), True, True,
               ["jf", "Hk"], w=["b%d" % (1 + hb)])
        for h in range(8):
            act(Et[:, h, :], banks[1 + h // 4][:, (h % 4) * 128:(h % 4 + 1) * 128], AF.Identity,
                ["b%d" % (1 + h // 4), "negb15"], pw=[nm], scale=128.0 ** 0.5, bias=negb15[:, h:h + 1])
    P.barrier()
    A.reset(m_const)

    def proj_pass(v0, blocks):
        m0 = A.mark()
        hT = A.t("hT", [128, KC, 2048], BF16)
        xin = [A.t("xin", [128, D], F32) for _ in range(3)]
        xn = [A.t("xn", [128, D], BF16) for _ in range(3)]
        mv = [A.t("mv", [128, 4], F32) for _ in range(3)]
        wst = [A.t("wst", [128, 8, 512], F32) for _ in range(2)]
        wb = [A.t("wb", [128, KC, 512], BF16) for _ in range(2)]
        ostf = [A.t("ostf", [128, 512], F32) for _ in range(4)]
        ostb = [A.t("ostb", [128, 512], BF16) for _ in range(4)]
        st2 = [A.t("st2", [128, 6], F32) for _ in range(2)]
        mv2 = [A.t("mv2", [128, 4], F32) for _ in range(2)]
        w_view = w_in.rearrange("(kc p) c -> p kc c", p=128)

        def load(j):
            c0, n = blocks[j][0], blocks[j][1]
            for hf in range(2):
                dma(wst[hf][:, :, 0:n], w_view[:, hf * 8:(hf + 1) * 8, c0:c0 + n], "wst%d" % hf, w=["wst%d" % hf])

        def cast(j):
            n = blocks[j][1]
            for hf in range(2):
                cp("pool", wb[j % 2][:, hf * 8:(hf + 1) * 8, 0:n], wst[hf][:, :, 0:n], ["wst%d" % hf], pw=["wb%d" % (j % 2)])

        load(0)
        cast(0)
        for t in range(16):
            b = t % 3
            dma(xin[b][:], xv[v0 + t * 128:v0 + (t + 1) * 128, :], "xin%d" % b, w=["xin%d" % b])
            P.add("act", (lambda b: lambda e: e.activation(out=xn[b][:], in_=xin[b][:], func=AF.Square, accum_out=mv[b][:, 2:3]))(b),
                  r=["xin%d" % b], w=["mvb%d" % b, "xn%d" % b])
            act(mv[b][:, 3:4], mv[b][:, 2:3], AF.Sqrt, ["mvb%d" % b], w=["mvc%d" % b], scale=1.0 / D, bias=EPS)
            P.add("dve", (lambda b: lambda e: e.reciprocal(out=mv[b][:, 3:4], in_=mv[b][:, 3:4]))(b), r=["mvc%d" % b], w=["mvc%d" % b])
            act(xn[b][:], xin[b][:], AF.Copy, ["xin%d" % b, "mvc%d" % b], w=["xn%d" % b], scale=mv[b][:, 3:4])
            for half in range(2):
                bk = 2 * (t % 2) + half
                for j in range(8):
                    kc = half * 8 + j
                    tr(bankb[bk][:, j * 128:(j + 1) * 128], xn[b][:, kc * 128:(kc + 1) * 128], ident[:],
                       ["xn%d" % b, "ident"], w=["b%d.%d" % (bk, j)])
                tt("dve", hT[:, half * 8:(half + 1) * 8, t * 128:(t + 1) * 128],
                   bankb[bk].rearrange("p (a b) -> p a b", a=8),
                   gainT[:, half * 8:(half + 1) * 8].unsqueeze(2).to_broadcast([128, 8, 128]), ALU.mult,
                   ["b%d.%d" % (bk, j) for j in range(8)] + ["gainT"], pw=["hT%d" % t])
        hT_all = ["hT%d" % t for t in range(16)]

        slot = [0]
        evrr = [0]

        def compute(j):
            c0, n, kind, dst = blocks[j]
            wbj = wb[j % 2]
            wres = "wb%d" % (j % 2)
            if kind == "fm":
                for sub in range(n // 128):
                    for tb in range(4):
                        bk = nextbank(4, 8)
                        for kc in range(KC):
                            mm(banks[bk][:, :], wbj[:, kc, sub * 128:(sub + 1) * 128], hT[:, kc, tb * 512:(tb + 1) * 512],
                               kc == 0, kc == KC - 1, [wres] + hT_all[tb * 4:tb * 4 + 4],
                               w=["b%d" % bk] if kc == 0 else (), pw=() if kc == 0 else ["b%d" % bk])
                        s = slot[0] % 4
                        slot[0] += 1
                        eng = "act" if evrr[0] % 2 == 0 else "dve"
                        evrr[0] += 1
                        cp(eng, ostb[s][:], banks[bk][:, :], ["b%d" % bk], w=["ostb%d" % s])
                        dap, idx0 = dst
                        dma(dap[idx0 + sub, :, v0 + tb * 512:v0 + (tb + 1) * 512] if dap is S_akT else
                            dap[idx0 + sub, :, tb * 512:(tb + 1) * 512], ostb[s][:], "ostb%d" % s, r=["ostb%d" % s])
                return
            for t in range(16):
                bk = nextbank(4, 8)
                for kc in range(KC):
                    mm(banks[bk][:, 0:n], hT[:, kc, t * 128:(t + 1) * 128], wbj[:, kc, 0:n], kc == 0, kc == KC - 1,
                       [wres, "hT%d" % t], w=["b%d" % bk] if kc == 0 else (), pw=() if kc == 0 else ["b%d" % bk])
                s = slot[0] % 4
                slot[0] += 1
                ps = banks[bk][:, 0:n]
                if kind == "ikw":
                    b2 = t % 2
                    P.add("dve", (lambda b2, bk: lambda e: e.bn_stats(out=st2[b2][:], in_=banks[bk][:, 0:64]))(b2, bk),
                          r=["b%d" % bk], w=["st2%d" % b2])
                    P.add("dve", (lambda b2: lambda e: e.bn_aggr(out=mv2[b2][:, 0:2], in_=st2[b2][:]))(b2),
                          r=["st2%d" % b2], w=["mv2%d" % b2])
                    act(mv2[b2][:, 2:3], mv2[b2][:, 1:2], AF.Sqrt, ["mv2%d" % b2], w=["mv2b%d" % b2], bias=EPS)
                    P.add("dve", (lambda b2: lambda e: e.reciprocal(out=mv2[b2][:, 2:3], in_=mv2[b2][:, 2:3]))(b2),
                          r=["mv2b%d" % b2], w=["mv2b%d" % b2])
                    ts("dve", ostb[s][:, 0:64], banks[bk][:, 0:64], mv2[b2][:, 0:1], mv2[b2][:, 2:3], ALU.subtract, ALU.mult,
                       ["b%d" % bk, "mv2%d" % b2, "mv2b%d" % b2], w=["ostb%d" % s])
                    dma(S_ikn[v0 + t * 128:v0 + (t + 1) * 128, :], ostb[s][:, 0:64], "ostb%d" % s, r=["ostb%d" % s])
                    if dst is not None:
                        ts("dve", ostf[s][:, 0:16], banks[bk][:, 64:80], 1.0 / 32.0, None, ALU.mult, None, ["b%d" % bk], w=["ostf%d" % s])
                        dma(S_iw[t * 128:(t + 1) * 128, :], ostf[s][:, 0:16], "ostf%d" % s, r=["ostf%d" % s])
                    continue
                dap, coff, isv = dst
                row0 = (v0 if isv else 0) + t * 128
                if kind == "f32":
                    eng = "act" if evrr[0] % 2 == 0 else "dve"
                    evrr[0] += 1
                    cp(eng, ostf[s][:, 0:n], ps, ["b%d" % bk], w=["ostf%d" % s])
                    dma(dap[row0:row0 + 128, coff:coff + n], ostf[s][:, 0:n], "ostf%d" % s, r=["ostf%d" % s])
                else:
                    if kind == "bf":
                        eng = "act" if evrr[0] % 2 == 0 else "dve"
                        evrr[0] += 1
                        cp(eng, ostb[s][:, 0:n], ps, ["b%d" % bk], w=["ostb%d" % s])
                    else:
                        act(ostb[s][:, 0:n], ps, AF.Silu if kind == "silu" else AF.Sigmoid, ["b%d" % bk], w=["ostb%d" % s])
                    dma(dap[row0:row0 + 128, coff:coff + n], ostb[s][:, 0:n], "ostb%d" % s, r=["ostb%d" % s])

        for j in range(len(blocks)):
            if j + 1 < len(blocks):
                load(j + 1)
            compute(j)
            if j + 1 < len(blocks):
                cast(j + 1)
        P.barrier()
        A.reset(m0)

    def tmblocks(c0, width, kind, dap, isv):
        return [(c0 + i * 512, 512, kind, (dap, i * 512, isv)) for i in range(width // 512)]

    ctx_blocks = ([(C_AK + i * 512, 512, "fm", (S_akT, i * 4)) for i in range(2)]
                  + tmblocks(C_AV, 1024, "bf", S_av, True)
                  + [(C_IK, 80, "ikw", None)]
                  + tmblocks(C_RK, 1024, "f32", S_rk, True)
                  + tmblocks(C_RV, 1024, "bf", S_rv, True))
    own_blocks = ([(C_AQ + i * 512, 512, "fm", (S_aqT, i * 4)) for i in range(2)]
                  + [(C_AK + i * 512, 512, "fm", (S_akT, i * 4)) for i in range(2)]
                  + tmblocks(C_AV, 1024, "bf", S_av, True)
                  + tmblocks(C_AZ, 1024, "silu", S_saz, False)
                  + [(C_IQ + i * 512, 512, "fm", (S_iqT, i * 4)) for i in range(2)]
                  + [(C_IK, 80, "ikw", True)]
                  + tmblocks(C_RQ, 1024, "f32", S_rq, False)
                  + tmblocks(C_RK, 1024, "f32", S_rk, True)
                  + tmblocks(C_RV, 1024, "bf", S_rv, True)
                  + tmblocks(C_RZ, 1024, "silu", S_srz, False)
                  + tmblocks(C_GA, 2048, "sigmoid", S_sga, False)
                  + tmblocks(C_GB, 2048, "sigmoid", S_sgb, False))
    if stop_after >= 1 and (phases is None or 1 in phases):
        proj_pass(0, ctx_blocks)
        proj_pass(2048, own_blocks)

    def retention():
        m0 = A.mark()
        gnb = A.t("gnb", [128, 1024], F32)
        dma(gnb[:], gn_d.partition_broadcast(128), "gnb", w=["gnb"])
        state = A.t("state", [128, 8, 128], F32)
        statebf = A.t("statebf", [128, 8, 128], BF16)
        memset("dve", state[:], 0.0, ["state"])
        memset("dve", statebf[:], 0.0, ["statebf"])
        NV = 32
        cs_all = A.t("cs_all", [128, NV, 64], F32)
        sn_all = A.t("sn_all", [128, NV, 64], F32)
        csk_all = A.t("csk_all", [128, NV, 64], F32)
        snk_all = A.t("snk_all", [128, NV, 64], F32)
        PI = math.pi
        m1 = A.mark()
        posTi = A.t("posTi", [128, NV], I32)
        posTf = A.t("posTf", [128, NV], F32)
        ang = A.t("ang", [128, NV, 64], F32)
        kfi = A.t("kfi", [128, NV * 64], I32)
        kf = A.t("kf", [128, NV * 64], F32)
        rr = A.t("rr", [128, NV * 64], F32)
        yy = A.t("yy", [128, NV * 64], F32)
        wm = A.t("wm", [128, NV * 64], F32)
        zz = A.t("zz", [128, NV * 64], F32)
        pacc = A.t("pacc", [128, NV * 64], F32)
        angf = ang[:].rearrange("p a b -> p (a b)")
        dma(posTi[:], posT_d, "posTi", w=["posTi"])
        cp("dve", posTf[:], posTi[:], ["posTi"], w=["posTf"])
        tt("dve", ang[:], cf[:, CF_IF:CF_IF + 64].unsqueeze(1).to_broadcast([128, NV, 64]),
           posTf[:].unsqueeze(2).to_broadcast([128, NV, 64]), ALU.mult, ["cf", "posTf"], w=["ang"])
        ts("dve", kf[:], angf, 1.0 / (2 * PI), None, ALU.mult, None, ["ang"], w=["kf"])
        cp("dve", kfi[:], kf[:], ["kf"], w=["kfi"])
        cp("dve", kf[:], kfi[:], ["kfi"], w=["kf"])
        stt(rr[:], kf[:], -2 * PI, angf, ALU.mult, ALU.add, ["kf", "ang"], w=["rr"])
        SC = [(-1.0) ** k / math.factorial(2 * k + 1) for k in range(8)]

        def psin(dst, dres):
            tt("dve", zz[:], yy[:], yy[:], ALU.mult, ["yy"], w=["zz"])
            ts("dve", pacc[:], zz[:], SC[7], None, ALU.mult, None, ["zz"], w=["pacc"])
            for k in (6, 5, 4, 3, 2, 1):
                stt(pacc[:], pacc[:], SC[k], zz[:], ALU.add, ALU.mult, ["pacc", "zz"], w=["pacc"])
            stt(dst[:].rearrange("p a b -> p (a b)"), pacc[:], 1.0, yy[:], ALU.add, ALU.mult, ["pacc", "yy"], w=[dres])

        ts("dve", yy[:], rr[:], -PI, PI, ALU.max, ALU.min, ["rr"], w=["yy"])
        psin(sn_all, "sn_all")
        ts("dve", yy[:], rr[:], PI / 2, None, ALU.add, None, ["rr", "sn_all"], w=["yy"])
        ts("dve", wm[:], yy[:], PI, -2 * PI, ALU.is_gt, ALU.mult, ["yy"], w=["wm"])
        tt("dve", yy[:], yy[:], wm[:], ALU.add, ["yy", "wm"], w=["yy"])
        ts("dve", yy[:], yy[:], -PI, PI, ALU.max, ALU.min, ["yy"], w=["yy"])
        psin(cs_all, "cs_all")
        sc = 128.0 ** -0.5
        ts("pool", csk_all[:], cs_all[:], sc, None, ALU.mult, None, ["cs_all"], w=["csk_all"])
        ts("pool", snk_all[:], sn_all[:], sc, None, ALU.mult, None, ["sn_all"], w=["snk_all"])
        P.barrier()
        A.reset(m1)
        rk32 = [A.t("rk32", [128, 1024], F32) for _ in range(3)]
        rq32 = [A.t("rq32", [128, 1024], F32) for _ in range(3)]
        rv16 = [A.t("rv16", [128, 1024], BF16) for _ in range(3)]
        srz = [A.t("srz", [128, 1024], BF16) for _ in range(3)]
        tmp4 = [[A.t("tmp4", [128, 8, 64], F32) for _ in range(4)] for _ in range(2)]
        rkr = [A.t("rkr", [128, 1024], BF16) for _ in range(2)]
        rkte = [A.t("rkte", [128, 1024], BF16) for _ in range(2)]
        rqr = [A.t("rqr", [128, 1024], BF16) for _ in range(2)]
        rkT = [A.t("rkT", [128, 8, 128], BF16) for _ in range(2)]
        rqT = [A.t("rqT", [128, 8, 128], BF16) for _ in range(2)]
        rqsT = [A.t("rqsT", [128, 8, 128], BF16) for _ in range(2)]
        Sm = [A.t("Sm", [128, 8, 128], BF16) for _ in range(2)]
        stg = [A.t("stg", [128, 8, 6], F32) for _ in range(2)]
        mvg = [A.t("mvg", [128, 8, 2], F32) for _ in range(2)]
        rstd = [A.t("rstd", [128, 8], F32) for _ in range(2)]
        y32 = [A.t("y32", [128, 1024], F32) for _ in range(2)]
        bo = [A.t("bo", [128, 1024], BF16) for _ in range(2)]

        def rope(x32, xres, cT, sT, cres, out, ores, b, meng="pool"):
            xv4 = x32[:].rearrange("p (h t d) -> p h t d", h=8, t=2)
            ov4 = out[:].rearrange("p (h t d) -> p h t d", h=8, t=2)
            x1, x2 = xv4[:, :, 0, :], xv4[:, :, 1, :]
            cb_ = cT.unsqueeze(1).to_broadcast([128, 8, 64])
            sb_ = sT.unsqueeze(1).to_broadcast([128, 8, 64])
            tn = ["tmp4%d_%d" % (b, k) for k in range(4)]
            tt(meng, tmp4[b][0][:], x1, cb_, ALU.mult, [xres] + cres, w=[tn[0]])
            tt(meng, tmp4[b][1][:], x2, sb_, ALU.mult, [xres] + cres, w=[tn[1]])
            tt("dve", ov4[:, :, 0, :], tmp4[b][0][:], tmp4[b][1][:], ALU.subtract, [tn[0], tn[1]], w=[ores])
            tt(meng, tmp4[b][2][:], x1, sb_, ALU.mult, [xres] + cres, w=[tn[2]])
            tt(meng, tmp4[b][3][:], x2, cb_, ALU.mult, [xres] + cres, w=[tn[3]])
            tt("dve", ov4[:, :, 1, :], tmp4[b][2][:], tmp4[b][3][:], ALU.add, [tn[2], tn[3]], pw=[ores])

        vl = list(vlist) if vlist is not None else list(range(32))

        def loads(v):
            own = v >= 16
            o = v - 16
            c = v % 3
            dma(rk32[c][:], S_rk[v * 128:(v + 1) * 128, :], "rk32%d" % c, w=["rk32%d" % c])
            dma(rv16[c][:], S_rv[v * 128:(v + 1) * 128, :], "rv16%d" % c, w=["rv16%d" % c])
            if own:
                dma(rq32[c][:], S_rq[o * 128:(o + 1) * 128, :], "rq32%d" % c, w=["rq32%d" % c])
                dma(srz[c][:], S_srz[o * 128:(o + 1) * 128, :], "srz%d" % c, w=["srz%d" % c])

        for n_ in range(min(2, len(vl))):
            loads(vl[n_])
        for idx, v in enumerate(vl):
            if idx + 2 < len(vl):
                loads(vl[idx + 2])
            own = v >= 16
            o = v - 16
            b = v % 2
            c = v % 3
            rope(rk32[c], "rk32%d" % c, csk_all[:, v, :], snk_all[:, v, :], [], rkr[b], "rkr%d" % b, b)
            tt("pool", rkte[b][:].rearrange("p (h d) -> p h d", h=8), rkr[b][:].rearrange("p (h d) -> p h d", h=8),
               cf[:, CF_TE:CF_TE + 8].unsqueeze(2).to_broadcast([128, 8, 128]), ALU.mult, ["rkr%d" % b], w=["rkte%d" % b])
            if own:
                rope(rq32[c], "rq32%d" % c, cs_all[:, v, :], sn_all[:, v, :], [], rqr[b], "rqr%d" % b, b, meng="dve")
                for h in range(8):
                    tr(bankb[0][:, h * 128:(h + 1) * 128], rkr[b][:, h * 128:(h + 1) * 128], ident[:], ["rkr%d" % b],
                       w=["b0"] if h == 0 else (), pw=() if h == 0 else ["b0"])
                for h in range(8):
                    tr(bankb[1][:, h * 128:(h + 1) * 128], rqr[b][:, h * 128:(h + 1) * 128], ident[:], ["rqr%d" % b],
                       w=["b1"] if h == 0 else (), pw=() if h == 0 else ["b1"])
                cp("dve", rkT[b][:].rearrange("p a b -> p (a b)"), bankb[0][:, :], ["b0"], w=["rkT%d" % b])
                cp("dve", rqT[b][:].rearrange("p a b -> p (a b)"), bankb[1][:, :], ["b1"], w=["rqT%d" % b])
                tt("dve", rqsT[b][:].rearrange("p a b -> p (a b)"), bankb[1][:, :], cf[:, CF_FS:CF_FS + 1024], ALU.mult,
                   ["b1"], w=["rqsT%d" % b])
                for h in range(8):
                    bk = 2 + h // 4
                    mm(banks[bk][:, (h % 4) * 128:(h % 4 + 1) * 128], rkT[b][:, h, :], rqT[b][:, h, :], True, True,
                       ["rkT%d" % b, "rqT%d" % b], w=["b%d" % bk] if h % 4 == 0 else (), pw=() if h % 4 == 0 else ["b%d" % bk])
                for hb in range(2):
                    tt("dve", Sm[b][:, hb * 4:(hb + 1) * 4, :].rearrange("p a b -> p (a b)"), banks[2 + hb][:, :],
                       cf[:, CF_M + hb * 512:CF_M + (hb + 1) * 512], ALU.mult, ["b%d" % (2 + hb)],
                       w=["Sm%d" % b] if hb == 0 else (), pw=() if hb == 0 else ["Sm%d" % b])
                for h in range(8):
                    bk = 4 + h // 4
                    oc = banks[bk][:, (h % 4) * 128:(h % 4 + 1) * 128]
                    mm(oc, Sm[b][:, h, :], rv16[c][:, h * 128:(h + 1) * 128], True, False, ["Sm%d" % b, "rv16%d" % c],
                       w=["b%d" % bk] if h % 4 == 0 else (), pw=() if h % 4 == 0 else ["b%d" % bk])
                    mm(oc, rqsT[b][:, h, :], statebf[:, h, :], False, True, ["rqsT%d" % b, "statebf"], pw=["b%d" % bk])
                for h in range(8):
                    bk = 4 + h // 4
                    oc = banks[bk][:, (h % 4) * 128:(h % 4 + 1) * 128]
                    P.add("dve", (lambda h, oc, b: lambda e: e.bn_stats(out=stg[b][:, h, :], in_=oc))(h, oc, b), r=["b%d" % bk],
                          w=["stg%d" % b] if h == 0 else (), pw=() if h == 0 else ["stg%d" % b])
                for h in range(8):
                    P.add("dve", (lambda h, b: lambda e: e.bn_aggr(out=mvg[b][:, h, :], in_=stg[b][:, h, :]))(h, b), r=["stg%d" % b],
                          w=["mvg%d" % b] if h == 0 else (), pw=() if h == 0 else ["mvg%d" % b])
                act(rstd[b][:], mvg[b][:, :, 1], AF.Sqrt, ["mvg%d" % b], w=["rstd%d" % b], bias=EPS)
                P.add("dve", (lambda b: lambda e: e.reciprocal(out=rstd[b][:], in_=rstd[b][:]))(b), r=["rstd%d" % b], w=["rstd%d" % b])
                for h in range(8):
                    bk = 4 + h // 4
                    oc = banks[bk][:, (h % 4) * 128:(h % 4 + 1) * 128]
                    ts("dve", y32[b][:, h * 128:(h + 1) * 128], oc, mvg[b][:, h, 0:1], rstd[b][:, h:h + 1], ALU.subtract, ALU.mult,
                       ["b%d" % bk, "mvg%d" % b, "rstd%d" % b], w=["y32%d" % b] if h == 0 else (), pw=() if h == 0 else ["y32%d" % b])
                tt("pool", y32[b][:], y32[b][:], gnb[:], ALU.mult, ["y32%d" % b, "gnb"], w=["y32%d" % b])
                tt("pool", bo[b][:], y32[b][:], srz[c][:], ALU.mult, ["y32%d" % b, "srz%d" % c], w=["bo%d" % b])
                dma(S_bo[o * 128:(o + 1) * 128, :], bo[b][:], "bo%d" % b, r=["bo%d" % b])
            for h in range(8):
                bk = 6 + h // 4
                mm(banks[bk][:, (h % 4) * 128:(h % 4 + 1) * 128], rkte[b][:, h * 128:(h + 1) * 128], rv16[c][:, h * 128:(h + 1) * 128],
                   True, True, ["rkte%d" % b, "rv16%d" % c], w=["b%d" % bk] if h % 4 == 0 else (), pw=() if h % 4 == 0 else ["b%d" % bk])
            for h in range(8):
                bk = 6 + h // 4
                stt(state[:, h, :], state[:, h, :], g128[h], banks[bk][:, (h % 4) * 128:(h % 4 + 1) * 128], ALU.mult, ALU.add,
                    ["state", "b%d" % bk], w=["state"])
            cp("act", statebf[:], state[:], ["state"], w=["statebf"])
        P.barrier()
        A.reset(m0)

    if stop_after >= 2 and (phases is None or 2 in phases):
        retention()

    def indexer():
        m0 = A.mark()
        kiT2 = A.t("kiT2", [128, NVIRT], BF16)
        ikl_all = A.t("ikl_all", [128, 32, 128], BF16)
        iksrc = S_ikn.rearrange("(v p) d -> p v d", p=128)
        for g4 in range(4):
            vs = slice(g4 * 8, (g4 + 1) * 8)
            dma(ikl_all[:, vs, 0:64], iksrc[:, vs, :], "ikA%d" % g4, pw=["ikl_all"])
            dma(ikl_all[:, vs, 64:128], iksrc[:, vs, :], "ikB%d" % g4, pw=["ikl_all"])
        for g4 in range(4):
            bk = g4 % 2
            for jj in range(8):
                tr(bankb[bk][:, jj * 128:(jj + 1) * 128], ikl_all[:, g4 * 8 + jj, :], ident[:], ["ikl_all", "ident"],
                   w=["b%d" % bk] if jj == 0 else (), pw=() if jj == 0 else ["b%d" % bk])
            cp("dve", kiT2[:, g4 * 1024:(g4 + 1) * 1024], bankb[bk][:, :], ["b%d" % bk], pw=["kiT2"])
        NB = 4
        acc = [A.t("acc", [128, NVIRT], F32) for _ in range(NB)]
        mx8 = [A.t("mx8", [128, 8], F32) for _ in range(NB)]
        bm = [A.t("bm", [128, 1], F32) for _ in range(NB)]
        bcnt = [A.t("bcnt", [128, 1], F32) for _ in range(NB)]
        bg = [A.t("bg", [128, 1], F32) for _ in range(NB)]
        mask = [A.t("mask", [128, NVIRT], BF16) for _ in range(NB)]
        maskT = [A.t("maskT", [128, NVIRT], BF16) for _ in range(2)]
        qiTz = [A.t("qiTz", [128, 16, 128], BF16) for _ in range(2)]
        iwt = [A.t("iwt", [128, 16], F32) for _ in range(2)]
        Dg = [A.t("Dg", [128, 16, 128], BF16) for _ in range(2)]
        tmp = [A.t("rtmp", [128, 512], BF16) for _ in range(6)]
        for q in range(2):
            memset("pool", qiTz[q][:], 0.0, ["qiTz%d" % q])
        trr = [0]
        RANGE = 64.0
        NIT = 30
        ACT_RELU_HEADS = 11

        tinfo = {}

        def score(i, b):
            N = 2048 + 128 * (i + 1)
            q = i % 2
            src = S_iqT[:, :, i * 128:(i + 1) * 128].rearrange("a p q -> p a q")
            qz4 = qiTz[q][:].rearrange("p (a two) q -> p a two q", two=2)
            dma(qz4[0:64, :, 0, :], src[0:64, :, :], "qiTa%d" % q, w=["qiTz%d" % q])
            dma(qz4[64:128, :, 1, :], src[64:128, :, :], "qiTb%d" % q, pw=["qiTz%d" % q])
            dma(iwt[q][:], S_iw[i * 128:(i + 1) * 128, :], "iwt%d" % q, w=["iwt%d" % q])
            tt("pool", Dg[q][:], ident[:].unsqueeze(1).to_broadcast([128, 16, 128]),
               iwt[q][:].unsqueeze(2).to_broadcast([128, 16, 128]), ALU.mult, ["iwt%d" % q, "ident"], w=["Dg%d" % q])
            nblk = (N + 511) // 512
            for blk in range(nblk):
                c0 = blk * 512
                wd = min(512, N - c0)
                isctx = c0 < 2048
                ares = "acc%d_%d" % (b, blk)
                ib = 2 + blk % 2
                slots = {}

                def smm(h):
                    bk = nextbank(4, 8)
                    mm(banks[bk][:, 0:wd], qiTz[q][:, h, :], kiT2[:, c0:c0 + wd], True, True, ["qiTz%d" % q, "kiT2"], w=["b%d" % bk])
                    sl = trr[0] % 6
                    trr[0] += 1
                    slots[h] = sl
                    if h < ACT_RELU_HEADS:
                        act(tmp[sl][:, 0:wd], banks[bk][:, 0:wd], AF.Relu, ["b%d" % bk], w=["rtmp%d" % sl])
                    else:
                        ts("dve", tmp[sl][:, 0:wd], banks[bk][:, 0:wd], 0.0, None, ALU.max, None, ["b%d" % bk], w=["rtmp%d" % sl])

                def dmm(h):
                    sl = slots[h]
                    mm(banks[ib][:, 0:wd], Dg[q][:, h, :], tmp[sl][:, 0:wd], h == 0, h == 15, ["Dg%d" % q, "rtmp%d" % sl],
                       w=["b%d" % ib] if h == 0 else (), pw=() if h == 0 else ["b%d" % ib])

                LA = 3
                for t in range(16 + LA):
                    if t < 16:
                        smm(t)
                    if t - LA >= 0:
                        dmm(t - LA)
                    yield
                if isctx:
                    act(acc[b][:, c0:c0 + wd], banks[ib][:, 0:wd], AF.Identity, ["b%d" % ib, "cmask"], w=[ares], bias=cmask[:, 0:1])
                else:
                    act(acc[b][:, c0:c0 + wd], banks[ib][:, 0:wd], AF.Identity, ["b%d" % ib], w=[ares])
            allacc = ["acc%d_%d" % (b, k) for k in range(nblk)]
            ts("dve", acc[b][:, N - 64:N], acc[b][:, N - 64:N], cf[:, CF_HMA:CF_HMA + 1], None, ALU.add, None, allacc + ["cf"],
               w=["accall%d" % b])
            tinfo[b] = (N, allacc)

        mtr = [0]

        def score_group(g):
            for k in range(2):
                yield from score(2 * g + k, 2 * (g % 2) + k)

        def bisect_group(g):
            bs = [2 * (g % 2), 2 * (g % 2) + 1]
            dve_set, act_set = [bs[0]], [bs[1]]
            for b in bs:
                N = tinfo[b][0]
                P.add("dve", (lambda b, N: lambda e: e.max(out=mx8[b][:], in_=acc[b][:, 0:N]))(b, N), r=["accall%d" % b], w=["mx8%d" % b])
                if b in dve_set:
                    ts("dve", bm[b][:], mx8[b][:, 0:1], -RANGE / 2, None, ALU.add, None, ["mx8%d" % b], w=["bm%d" % b])
                else:
                    ts("dve", bm[b][:], mx8[b][:, 0:1], -1.0, RANGE / 2, ALU.mult, ALU.add, ["mx8%d" % b], w=["bm%d" % b])
            wk = RANGE / 2
            for it in range(NIT):
                last = it == NIT - 1
                for b in dve_set:
                    N = tinfo[b][0]
                    P.add("dve", (lambda b, N: lambda e: e.tensor_scalar(out=mask[b][:, 0:N], in0=acc[b][:, 0:N], scalar1=bm[b][:, 0:1],
                                                                         scalar2=None, op0=ALU.is_ge, op1=ALU.add, accum_out=bcnt[b][:]))(b, N),
                          r=["accall%d" % b, "bm%d" % b], w=["bcnt%d" % b, "mask%d" % b])
                for b in act_set:
                    N = tinfo[b][0]
                    P.add("act", (lambda b, N: lambda e: e.activation(out=mask[b][:, 0:N], in_=acc[b][:, 0:N], func=AF.Sign, bias=bm[b][:, 0:1],
                                                                      scale=1.0, accum_out=bcnt[b][:]))(b, N),
                          r=["accall%d" % b, "bm%d" % b], w=["bcnt%d" % b, "mask%d" % b])
                yield
                for b in dve_set:
                    ts("dve", bg[b][:], bcnt[b][:], 255.5, 1.0 if last else 0.5, ALU.is_ge, ALU.subtract, ["bcnt%d" % b], w=["bg%d" % b])
                    stt(bm[b][:], bg[b][:], wk, bm[b][:], ALU.mult, ALU.add, ["bg%d" % b, "bm%d" % b], w=["bm%d" % b])
                for b in act_set:
                    N = tinfo[b][0]
                    act(bg[b][:], bcnt[b][:], AF.Sign, ["bcnt%d" % b], w=["bg%d" % b], bias=float(N - 511))
                    act(bm[b][:], bg[b][:], AF.Identity, ["bg%d" % b, "bm%d" % b], w=["bm%d" % b], scale=-wk / 2, bias=bm[b][:, 0:1])
                    if last:
                        act(bm[b][:], bm[b][:], AF.Identity, ["bm%d" % b], w=["bm%d" % b], bias=wk / 2)
                wk = wk / 2
                yield
            for b in act_set:
                ts("dve", bm[b][:], bm[b][:], -1.0, None, ALU.mult, None, ["bm%d" % b], w=["bm%d" % b])

        def mask_group(g):
            for k in range(2):
                b = 2 * (g % 2) + k
                i = 2 * g + k
                N, allacc = tinfo[b]
                ts("dve", mask[b][:, 0:N], acc[b][:, 0:N], bm[b][:, 0:1], None, ALU.is_ge, None, ["accall%d" % b, "bm%d" % b],
                   w=["mask%d" % b, "accfree%d" % b] + allacc)
                K = N // 128
                mq = mtr[0] % 2
                mtr[0] += 1
                for gg in range((K + 7) // 8):
                    kts = list(range(gg * 8, min(K, gg * 8 + 8)))
                    bk = gg % 2
                    for jj, kt in enumerate(kts):
                        tr(bankb[bk][:, jj * 128:(jj + 1) * 128], mask[b][:, kt * 128:(kt + 1) * 128], ident[:], ["mask%d" % b, "ident"],
                           w=["b%d" % bk] if jj == 0 else (), pw=() if jj == 0 else ["b%d" % bk])
                    ts("dve", maskT[mq][:, gg * 1024:gg * 1024 + len(kts) * 128], bankb[bk][:, 0:len(kts) * 128], -1.0, 2048.0, ALU.add, ALU.mult,
                       ["b%d" % bk], w=["maskT%d" % mq] if gg == 0 else (), pw=() if gg == 0 else ["maskT%d" % mq])
                dma(S_mT[i, :, 0:N], maskT[mq][:, 0:N], "maskT%d" % mq, r=["maskT%d" % mq])

        for _ in score_group(0):
            pass
        NG = 8
        for g in range(NG):
            bis = bisect_group(g)
            sc = score_group(g + 1) if g + 1 < NG else iter(())
            per = 6
            for _ in bis:
                for _k in range(per):
                    next(sc, None)
            for _ in sc:
                pass
            mask_group(g)
        P.barrier()
        A.reset(m0)

    def attention():
        m0 = A.mark()
        akT = A.t("akT", [128, 8, NVIRT], BF16)
        Vaug = A.t("Vaug", [128, 32, 8, 129], BF16)
        vst = [A.t("vst", [128, 1024], BF16) for _ in range(2)]
        for h in range(8):
            dma(akT[:, h, :], S_akT[h], "akT%d" % h, w=["akT%d" % h])
        akall = ["akT%d" % h for h in range(8)]
        memset("pool", Vaug[:, :, :, 128:129], 1.0, ["Vones"])
        for v in range(32):
            dma(Vaug[:, v, :, 0:128], S_av[v * 128:(v + 1) * 128, :].rearrange("p (h d) -> p h d", h=8), "vld%d" % (v % 8), pw=["Vaug"])
        aqT = [A.t("aqT", [128, 8, 128], BF16) for _ in range(2)]
        mT = [A.t("mT", [128, NVIRT], BF16) for _ in range(2)]
        saz = [A.t("saz", [128, 1024], BF16) for _ in range(2)]
        mE = [A.t("mE", [128, 8, 2, 128], BF16) for _ in range(2)]
        pt = [A.t("pt", [128, 512], BF16) for _ in range(4)]
        ao = [A.t("ao", [128, 1024], BF16) for _ in range(2)]
        rc = A.t("rc", [128, 8], F32)
        rot = [0]
        def qloads(i):
            N = (17 + i) * 128
            b = i % 2
            dma(aqT[b][:], S_aqT[:, :, i * 128:(i + 1) * 128].rearrange("a p q -> p a q"), "aqT%d" % b, w=["aqT%d" % b])
            dma(mT[b][:, 0:N], S_mT[i, :, 0:N], "mT%d" % b, w=["mT%d" % b])
            dma(saz[b][:], S_saz[i * 128:(i + 1) * 128, :], "saz%d" % b, w=["saz%d" % b])

        qloads(0)
        for i in range(16):
            K = 17 + i
            N = K * 128
            b = i % 2
            if i + 1 < 16:
                qloads(i + 1)
            kt0, kt1 = K - 1, K - 2
            tt("pool", mE[b][:, :, 0, :], E0[:], mT[b][:, kt0 * 128:(kt0 + 1) * 128].unsqueeze(1).to_broadcast([128, 8, 128]),
               ALU.add, ["mT%d" % b], w=["mE%d" % b])
            tt("pool", mE[b][:, :, 1, :], E1[:], mT[b][:, kt1 * 128:(kt1 + 1) * 128].unsqueeze(1).to_broadcast([128, 8, 128]),
               ALU.add, ["mT%d" % b], pw=["mE%d" % b])
            units = [(h, list(range(g * 4, min(K, g * 4 + 4)))) for h in range(8) for g in range((K + 3) // 4)]
            ubank, uslot = {}, {}

            def qk(u):
                h, grp = units[u]
                bk = nextbank(0, 6)
                ubank[u] = bk
                for j, kt in enumerate(grp):
                    oc = banks[bk][:, j * 128:(j + 1) * 128]
                    mm(oc, akT[:, h, kt * 128:(kt + 1) * 128], aqT[b][:, h, :], True, False,
                       ["akT%d" % h, "aqT%d" % b], w=["b%d" % bk] if j == 0 else (), pw=() if j == 0 else ["b%d" % bk])
                    if kt >= kt1:
                        mb, mres = mE[b][:, h, 0 if kt == kt0 else 1, :], "mE%d" % b
                    else:
                        mb, mres = mT[b][:, kt * 128:(kt + 1) * 128], "mT%d" % b
                    mm(oc, ident[:], mb, False, True, ["ident", mres], pw=["b%d" % bk])
                wd = len(grp) * 128
                sl = rot[0] % 4
                rot[0] += 1
                uslot[u] = sl
                act(pt[sl][:, 0:wd], banks[bk][:, 0:wd], AF.Exp, ["b%d" % bk, "tab"], w=["pt%d" % sl],
                    scale=128.0 ** -0.5, bias=tab[:, 120 + h:121 + h])

            def pvs(u):
                h, grp = units[u]
                pv = 6 + h % 2
                sl = uslot[u]
                for j, kt in enumerate(grp):
                    mm(banks[pv][:, 0:129], pt[sl][:, j * 128:(j + 1) * 128], Vaug[:, kt, h, :], kt == 0, kt == K - 1,
                       ["pt%d" % sl, "Vaug", "Vones"], w=["b%d" % pv] if kt == 0 else (), pw=() if kt == 0 else ["b%d" % pv])
                if grp[-1] == K - 1:
                    P.add("dve", (lambda h, pv: lambda e: e.reciprocal(out=rc[:, h:h + 1], in_=banks[pv][:, 128:129]))(h, pv),
                          r=["b%d" % pv], w=["rc%d" % h])
                    stt(ao[b][:, h * 128:(h + 1) * 128], banks[pv][:, 0:128], rc[:, h:h + 1], saz[b][:, h * 128:(h + 1) * 128],
                        ALU.mult, ALU.mult, ["b%d" % pv, "rc%d" % h, "saz%d" % b], w=["ao%d" % b] if h == 0 else (), pw=() if h == 0 else ["ao%d" % b])

            LA = 3
            for t in range(len(units) + LA):
                if t < len(units):
                    qk(t)
                if t - LA >= 0:
                    pvs(t - LA)
            dma(S_ao[i * 128:(i + 1) * 128, :], ao[b][:], "ao%d" % b, r=["ao%d" % b])
        P.barrier()
        A.reset(m0)

    def load_w(W_d, Wsb, kcn, wst4, nm):
        view = W_d.rearrange("(kc p) c -> p kc c", p=128)
        k = 0
        for cb in range(4):
            for k0 in range(0, kcn, 2):
                kk = min(2, kcn - k0)
                sl = k % 4
                k += 1
                dma(wst4[sl][:, 0:kk, :], view[:, k0:k0 + kk, cb * 512:(cb + 1) * 512], "wst4%d" % sl, w=["wst4%d" % sl])
                cp(("pool", "act", "dve")[k % 3], Wsb[:, k0:k0 + kk, cb * 512:(cb + 1) * 512], wst4[sl][:, 0:kk, :], ["wst4%d" % sl], pw=[nm])

    def transpose_tile(src, sres, kcn, dstT, dres, bk0):
        for g in range((kcn + 7) // 8):
            n = min(8, kcn - g * 8)
            bk = bk0 + g
            for j in range(n):
                kc = g * 8 + j
                tr(bankb[bk][:, j * 128:(j + 1) * 128], src[:, kc * 128:(kc + 1) * 128], ident[:], [sres, "ident"],
                   w=["b%d" % bk] if j == 0 else (), pw=() if j == 0 else ["b%d" % bk])
            cp("dve", dstT[:, g * 8:g * 8 + n, :].rearrange("p a b -> p (a b)"), bankb[bk][:, 0:n * 128], ["b%d" % bk],
               w=[dres] if g == 0 else (), pw=() if g == 0 else [dres])

    def merge():
        m0 = A.mark()
        Wa = A.t("Wa", [128, 8, D], BF16)
        Wb = A.t("Wb", [128, 8, D], BF16)
        wst4 = [A.t("wst4", [128, 2, 512], F32) for _ in range(4)]
        load_w(w_a, Wa, 8, wst4, "Wa")
        load_w(w_b, Wb, 8, wst4, "Wb")
        ao_t = [A.t("ao_t", [128, 1024], BF16) for _ in range(3)]
        bo_t = [A.t("bo_t", [128, 1024], BF16) for _ in range(3)]
        sga_t = [A.t("sga_t", [128, D], BF16) for _ in range(3)]
        sgb_t = [A.t("sgb_t", [128, D], BF16) for _ in range(3)]
        aoT = [A.t("aoT", [128, 8, 128], BF16) for _ in range(2)]
        boT = [A.t("boT", [128, 8, 128], BF16) for _ in range(2)]
        m1 = [A.t("m1", [128, 512], F32) for _ in range(2)]
        m2 = [A.t("m2", [128, 512], F32) for _ in range(2)]
        mg = [A.t("mg", [128, D], BF16) for _ in range(2)]
        kctr = [0]

        def stage0(t):
                c = t % 3
                rows = slice(t * 128, (t + 1) * 128)
                dma(ao_t[c][:], S_ao[rows, :], "ao_t%d" % c, w=["ao_t%d" % c])
                dma(bo_t[c][:], S_bo[rows, :], "bo_t%d" % c, w=["bo_t%d" % c])
                dma(sga_t[c][:], S_sga[rows, :], "sga_t%d" % c, w=["sga_t%d" % c])
                dma(sgb_t[c][:], S_sgb[rows, :], "sgb_t%d" % c, w=["sgb_t%d" % c])

        def stage1(t):
                b = t % 2
                c = t % 3
                transpose_tile(ao_t[c], "ao_t%d" % c, 8, aoT[b], "aoT%d" % b, 0)
                transpose_tile(bo_t[c], "bo_t%d" % c, 8, boT[b], "boT%d" % b, 1)

        def stage2(t):
                b = t % 2
                rows = slice(t * 128, (t + 1) * 128)
                for cb in range(4):
                    cols = slice(cb * 512, (cb + 1) * 512)
                    bkA = nextbank(2, 8)
                    for kc in range(8):
                        mm(banks[bkA][:, :], aoT[b][:, kc, :], Wa[:, kc, cols], kc == 0, kc == 7, ["aoT%d" % b, "Wa"],
                           w=["b%d" % bkA] if kc == 0 else (), pw=() if kc == 0 else ["b%d" % bkA])
                    bkB = nextbank(2, 8)
                    for kc in range(8):
                        mm(banks[bkB][:, :], boT[b][:, kc, :], Wb[:, kc, cols], kc == 0, kc == 7, ["boT%d" % b, "Wb"],
                           w=["b%d" % bkB] if kc == 0 else (), pw=() if kc == 0 else ["b%d" % bkB])
                    sl = kctr[0] % 2
                    kctr[0] += 1
                    tt("dve", m1[sl][:], banks[bkA][:, :], sga_t[t % 3][:, cols], ALU.mult, ["b%d" % bkA, "sga_t%d" % (t % 3)], w=["m1%d" % sl])
                    tt("dve", m2[sl][:], banks[bkB][:, :], sgb_t[t % 3][:, cols], ALU.mult, ["b%d" % bkB, "sgb_t%d" % (t % 3)], w=["m2%d" % sl])
                    tt("pool", mg[b][:, cols], m1[sl][:], m2[sl][:], ALU.add, ["m1%d" % sl, "m2%d" % sl],
                       w=["mg%d" % b] if cb == 0 else (), pw=() if cb == 0 else ["mg%d" % b])
                dma(S_mg[rows, :], mg[b][:], "mg%d" % b, r=["mg%d" % b])

        stage0(0)
        stage0(1)
        stage1(0)
        for t in range(16):
            if t + 2 < 16:
                stage0(t + 2)
            if t + 1 < 16:
                stage1(t + 1)
            stage2(t)
        P.barrier()
        A.reset(m0)

    def outproj():
        m0 = A.mark()
        Wo = A.t("Wo", [128, 16, D], BF16)
        wst4 = [A.t("wst4", [128, 2, 512], F32) for _ in range(4)]
        load_w(w_o, Wo, 16, wst4, "Wo")
        mg_t = [A.t("mg_t", [128, D], BF16) for _ in range(3)]
        x_t = [A.t("x_t", [128, D], F32) for _ in range(3)]
        mgT = [A.t("mgT", [128, 16, 128], BF16) for _ in range(2)]
        r_t = [A.t("r_t", [128, D], F32) for _ in range(2)]
        rb_t = [A.t("rb_t", [128, D], BF16) for _ in range(2)]
        def stage0(t):
                c = t % 3
                rows = slice(t * 128, (t + 1) * 128)
                dma(mg_t[c][:], S_mg[rows, :], "mg_t%d" % c, w=["mg_t%d" % c])
                dma(x_t[c][:], xv[2048 + t * 128:2048 + (t + 1) * 128, :], "x_t%d" % c, w=["x_t%d" % c])

        def stage1(t):
                b = t % 2
                c = t % 3
                transpose_tile(mg_t[c], "mg_t%d" % c, 16, mgT[b], "mgT%d" % b, 0)

        def stage2(t):
                b = t % 2
                rows = slice(t * 128, (t + 1) * 128)
                for cb in range(4):
                    cols = slice(cb * 512, (cb + 1) * 512)
                    bk = nextbank(2, 8)
                    for kc in range(16):
                        mm(banks[bk][:, :], mgT[b][:, kc, :], Wo[:, kc, cols], kc == 0, kc == 15, ["mgT%d" % b, "Wo"],
                           w=["b%d" % bk] if kc == 0 else (), pw=() if kc == 0 else ["b%d" % bk])
                    tt("dve", r_t[b][:, cols], banks[bk][:, :], x_t[t % 3][:, cols], ALU.add, ["b%d" % bk, "x_t%d" % (t % 3)],
                       w=["r_t%d" % b] if cb == 0 else (), pw=() if cb == 0 else ["r_t%d" % b])
                cp("pool", rb_t[b][:], r_t[b][:], ["r_t%d" % b], w=["rb_t%d" % b])
                dma(S_r[rows, :], r_t[b][:], "r_t%d" % b, r=["r_t%d" % b])
                dma(S_rb[rows, :], rb_t[b][:], "rb_t%d" % b, r=["rb_t%d" % b])

        stage0(0)
        stage0(1)
        stage1(0)
        for t in range(16):
            if t + 2 < 16:
                stage0(t + 2)
            if t + 1 < 16:
                stage1(t + 1)
            stage2(t)
        P.barrier()
        A.reset(m0)

    def final():
        m0 = A.mark()
        Wg = A.t("Wg", [128, 16, D], BF16)
        Wp = A.t("Wp", [128, 2, D], BF16)
        fgb = A.t("fgb", [128, D], F32)
        wst4 = [A.t("wst4", [128, 2, 512], F32) for _ in range(4)]
        load_w(w_g, Wg, 16, wst4, "Wg")
        load_w(w_p, Wp, 2, wst4, "Wp")
        dma(fgb[:], fg_d.partition_broadcast(128), "fgb", w=["fgb"])
        rb_t = [A.t("rb_t", [128, D], BF16) for _ in range(3)]
        r_t = [A.t("r_t", [128, D], F32) for _ in range(3)]
        p_t = [A.t("p_t", [128, 256], F32) for _ in range(3)]
        pb_t = [A.t("pb_t", [128, 256], BF16) for _ in range(3)]
        rT = [A.t("rT", [128, 16, 128], BF16) for _ in range(2)]
        pT = [A.t("pT", [128, 2, 128], BF16) for _ in range(2)]
        o_t = [A.t("o_t", [128, D], F32) for _ in range(2)]
        sg = [A.t("sg", [128, 512], F32) for _ in range(2)]
        t2 = [A.t("t2", [128, 512], F32) for _ in range(2)]
        st = [A.t("stf", [128, 4, 6], F32) for _ in range(2)]
        mv = [A.t("mvf", [128, 4], F32) for _ in range(2)]
        kctr = [0]

        def stage0(t):
                c = t % 3
                rows = slice(t * 128, (t + 1) * 128)
                dma(rb_t[c][:], S_rb[rows, :], "rb_t%d" % c, w=["rb_t%d" % c])
                dma(r_t[c][:], S_r[rows, :], "r_t%d" % c, w=["r_t%d" % c])
                dma(p_t[c][:], p_o[rows, :], "p_t%d" % c, w=["p_t%d" % c])
                cp("pool", pb_t[c][:], p_t[c][:], ["p_t%d" % c], w=["pb_t%d" % c])

        def stage1(t):
                b = t % 2
                c = t % 3
                transpose_tile(rb_t[c], "rb_t%d" % c, 16, rT[b], "rT%d" % b, 0)
                transpose_tile(pb_t[c], "pb_t%d" % c, 2, pT[b], "pT%d" % b, 2)

        def stage2(t):
                b = t % 2
                rows = slice(t * 128, (t + 1) * 128)
                for cb in range(4):
                    cols = slice(cb * 512, (cb + 1) * 512)
                    bkG = nextbank(3, 8)
                    for kc in range(16):
                        mm(banks[bkG][:, :], rT[b][:, kc, :], Wg[:, kc, cols], kc == 0, kc == 15, ["rT%d" % b, "Wg"],
                           w=["b%d" % bkG] if kc == 0 else (), pw=() if kc == 0 else ["b%d" % bkG])
                    bkP = nextbank(3, 8)
                    for kc in range(2):
                        mm(banks[bkP][:, :], pT[b][:, kc, :], Wp[:, kc, cols], kc == 0, kc == 1, ["pT%d" % b, "Wp"],
                           w=["b%d" % bkP] if kc == 0 else (), pw=() if kc == 0 else ["b%d" % bkP])
                    sl = kctr[0] % 2
                    kctr[0] += 1
                    act(sg[sl][:], banks[bkG][:, :], AF.Sigmoid, ["b%d" % bkG], w=["sg%d" % sl])
                    tt("dve", t2[sl][:], banks[bkP][:, :], sg[sl][:], ALU.mult, ["b%d" % bkP, "sg%d" % sl], w=["t2%d" % sl])
                    tt("pool", o_t[b][:, cols], t2[sl][:], r_t[t % 3][:, cols], ALU.add, ["t2%d" % sl, "r_t%d" % (t % 3)],
                       w=["o_t%d" % b] if cb == 0 else (), pw=() if cb == 0 else ["o_t%d" % b])
                for c in range(4):
                    P.add("dve", (lambda b, c: lambda e: e.bn_stats(out=st[b][:, c, :], in_=o_t[b][:, c * 512:(c + 1) * 512]))(b, c),
                          r=["o_t%d" % b], w=["stf%d" % b] if c == 0 else (), pw=() if c == 0 else ["stf%d" % b])
                P.add("dve", (lambda b: lambda e: e.bn_aggr(out=mv[b][:, 0:2], in_=st[b][:].rearrange("p a b -> p (a b)")))(b),
                      r=["stf%d" % b], w=["mvf%d" % b])
                stt(mv[b][:, 2:3], mv[b][:, 0:1], mv[b][:, 0:1], mv[b][:, 1:2], ALU.mult, ALU.add, ["mvf%d" % b], w=["mvfb%d" % b])
                act(mv[b][:, 3:4], mv[b][:, 2:3], AF.Sqrt, ["mvfb%d" % b], w=["mvfc%d" % b], bias=EPS)
                P.add("dve", (lambda b: lambda e: e.reciprocal(out=mv[b][:, 3:4], in_=mv[b][:, 3:4]))(b), r=["mvfc%d" % b], w=["mvfc%d" % b])
                stt(o_t[b][:], o_t[b][:], mv[b][:, 3:4], fgb[:], ALU.mult, ALU.mult, ["o_t%d" % b, "mvfc%d" % b, "fgb"], w=["o_t%d" % b])
                dma(out_d[rows, :], o_t[b][:], "o_t%d" % b, r=["o_t%d" % b])

        stage0(0)
        stage0(1)
        stage1(0)
        for t in range(16):
            if t + 2 < 16:
                stage0(t + 2)
            if t + 1 < 16:
                stage1(t + 1)
            stage2(t)
        P.barrier()
        A.reset(m0)

    if stop_after >= 3 and (phases is None or 3 in phases):
        indexer()
    if stop_after >= 4 and (phases is None or 4 in phases):
        attention()
    if stop_after >= 5 and (phases is None or 5 in phases):
        merge()
    if stop_after >= 6 and (phases is None or 6 in phases):
        outproj()
    if stop_after >= 7 and (phases is None or 7 in phases):
        final()
    return nc, es, P, locals()


_CACHE = {}


def kernel(x, p, positions, w_in, norm_gain, w_a_out, w_b_out, w_o, ret_gn_gain, w_ple, w_ple_gate, rel_bias, final_gain):
    if "nc" not in _CACHE:
        nc, es, P, _ = build()
        finish(nc, es, P)
        _CACHE["nc"] = nc
    nc = _CACHE["nc"]
    maps = make_in_maps(x, p, positions, w_in, norm_gain, w_a_out, w_b_out, w_o, ret_gn_gain, w_ple, w_ple_gate,
                        rel_bias, final_gain)
    res = run_bass_kernel_spmd(nc, maps, core_ids=list(range(8)))
    out = np.zeros((4, 4096, D), np.float32)
    for c in range(8):
        b, half = c // 2, c % 2
        out[b, half * 2048:(half + 1) * 2048] = np.asarray(res.results[c]["out"], dtype=np.float32)
    return out


def finish(nc, es, P):
    i = P.add("sp", None)
    P.ops[i]["deps"] = set(P.last_dma.values()) | set(P.last_eng.values())
    P.emit()
    es.close()
    return nc


def make_in_maps(x, p, positions, w_in, norm_gain, w_a_out, w_b_out, w_o, ret_gn_gain, w_ple, w_ple_gate,
                 rel_bias, final_gain, cores=range(8)):
    cf_np, o1_np, _ = host_consts()
    f = lambda a: np.ascontiguousarray(a, dtype=np.float32)
    shared = {
        "w_in": f(w_in[0]), "gainT": f(np.asarray(norm_gain[0]).reshape(KC, 128).T),
        "w_a_out": f(w_a_out[0]), "w_b_out": f(w_b_out[0]), "w_o": f(w_o[0]), "w_ple_gate": f(w_ple_gate[0]),
        "w_ple": f(w_ple[0]), "ret_gn_gain": f(np.asarray(ret_gn_gain[0]).reshape(1, 1024)),
        "rel_bias": f(rel_bias), "rel_row": f(np.asarray(rel_bias).reshape(1, 256)),
        "final_gain": f(np.asarray(final_gain).reshape(1, D)), "cf": cf_np, "o1": o1_np,
    }
    maps = []
    for c in cores:
        b, half = c // 2, c % 2
        xb = np.asarray(x[b])
        pb = np.asarray(positions[b]).astype(np.int32)
        if half == 1:
            xvv = f(xb)
            pv = pb.reshape(NVIRT, 1)
            cm = np.tile(np.array([[0.0, 1.0]], np.float32), (128, 1))
        else:
            xvv = np.zeros((NVIRT, D), np.float32)
            xvv[2048:] = xb[:2048]
            pv = np.zeros((NVIRT, 1), np.int32)
            pv[2048:, 0] = pb[:2048]
            cm = np.tile(np.array([[NEG, 0.0]], np.float32), (128, 1))
        m = dict(shared)
        m.update({"xv": xvv, "posv": np.ascontiguousarray(pv), "cmask": cm,
                  "posT": np.ascontiguousarray(pv.reshape(32, 128).T),
                  "p_o": f(np.asarray(p[0, b, half * 2048:(half + 1) * 2048]))})
        maps.append(m)
    return maps
```

```python
import math
from contextlib import ExitStack

import numpy as np
import ml_dtypes

import concourse.bass as bass
import concourse.mybir as mybir
from concourse.bass_utils import run_bass_kernel_spmd

F32 = mybir.dt.float32
BF16 = mybir.dt.bfloat16
I32 = mybir.dt.int32
AF = mybir.ActivationFunctionType
ALU = mybir.AluOpType

import os
MAXOPS = int(os.environ.get("MAXOPS", "100000000"))
ENGS = ["pe", "act", "pool", "dve", "sp"]
BLK = {"pe": "tensor", "act": "scalar", "pool": "gpsimd", "dve": "vector", "sp": "sync"}


class Prog:
    def __init__(self, nc, es):
        self.nc, self.es = nc, es
        self.ops = []
        self.W, self.R = {}, {}
        self.last_eng = {}
        self.last_dma = {}

    def add(self, eng, fn, r=(), w=(), pw=(), dma=None):
        i = len(self.ops)
        if i >= MAXOPS and fn is not None:
            return i
        deps = set()
        for x in r:
            deps.update(self.W.get(x, ()))
        for x in w:
            deps.update(self.W.get(x, ()))
            deps.update(self.R.get(x, ()))
        for x in pw:
            deps.update(self.R.get(x, ()))
        for x in r:
            self.R.setdefault(x, []).append(i)
        for x in w:
            self.W[x] = [i]
            self.R[x] = []
        for x in pw:
            if self.R.get(x):
                self.W[x] = [i]
                self.R[x] = []
            else:
                self.W.setdefault(x, []).append(i)
        self.ops.append(dict(eng=eng, fn=fn, deps=deps, dma=dma))
        if dma is None:
            self.last_eng[eng] = i
        else:
            self.last_dma[dma] = i
        return i

    def barrier(self):
        deps = set(self.last_eng.values()) | set(self.last_dma.values())
        for n, e in enumerate(ENGS):
            i = len(self.ops)
            self.ops.append(dict(eng=e, fn=None, deps=set(deps), dma=None, bar=(n == 0)))
        self.W, self.R = {}, {}

    def emit(self):
        nc, ops = self.nc, self.ops

        def skip(od, o):
            return od["dma"] is None and o["dma"] is None and od["eng"] == "pe" and o["eng"] == "pe"

        signaled = [False] * len(ops)
        for o in ops:
            for d in o["deps"]:
                if not skip(ops[d], o):
                    signaled[d] = True
        sems = {}

        def sem(name):
            if name not in sems:
                sems[name] = self.es.enter_context(nc.semaphore("s_" + name))
            return sems[name]

        cnt = {}
        cur, free, nd = {}, [], [0]
        for i, o in enumerate(ops):
            if o.get("bar"):
                free.extend(cur.values())
                cur = {}
            if o["fn"] is None:
                o["sig"] = None
            elif o["dma"] is not None:
                if o["dma"] not in cur:
                    if free:
                        cur[o["dma"]] = free.pop()
                    else:
                        cur[o["dma"]] = "d_%d" % nd[0]
                        nd[0] += 1
                k = cur[o["dma"]]
                cnt[k] = cnt.get(k, 0) + 16
                o["sig"] = (k, cnt[k])
            elif signaled[i]:
                k = "e_" + o["eng"]
                cnt[k] = cnt.get(k, 0) + 1
                o["sig"] = (k, cnt[k])
            else:
                o["sig"] = None
        self.nsems = len(cnt)
        for k in cnt:
            sem(k)
        block = self.es.enter_context(nc.Block())
        for e in ENGS:
            mine = [o for o in ops if o["eng"] == e]

            def body(eng, mine=mine):
                waited = {}
                for o in mine:
                    need = {}
                    for d in o["deps"]:
                        od = ops[d]
                        if od["sig"] is None or skip(od, o):
                            continue
                        k, v = od["sig"]
                        if need.get(k, 0) < v:
                            need[k] = v
                    for k, v in need.items():
                        if waited.get(k, 0) < v:
                            eng.wait_ge(sems[k], v)
                            waited[k] = v
                    if o["fn"] is None:
                        continue
                    ins = o["fn"](eng)
                    if o["sig"] is not None:
                        ins.then_inc(sems[o["sig"][0]], 16 if o["dma"] is not None else 1)

            getattr(block, BLK[e])(body)


D = 2048
KC = 16
NOWN = 2048
NVIRT = 4096
NEG = -1.0e30
C_AQ, C_AK, C_AV, C_AZ, C_IQ, C_IK, C_IW, C_RQ, C_RK, C_RV, C_RZ, C_GA, C_GB, C_END = (
    0, 1024, 2048, 3072, 4096, 5120, 5184, 5200, 6224, 7248, 8272, 9296, 11344, 13392)
EPS = 1e-6
T5_THR = [1, 2, 3, 4, 5, 6, 7, 8, 12, 16, 23, 32, 46, 64, 91]
CF_HMA, CF_HMM, CF_TE, CF_IF, CF_FS, CF_M, CF_N = 0, 1, 2, 10, 74, 74 + 1024, 74 + 2048


def _t5_bucket(rel):
    n = abs(rel)
    b = sum(1 for t in T5_THR if n >= t)
    return b + (16 if rel > 0 else 0)


def host_consts():
    cf = np.zeros((128, CF_N), np.float32)
    p = np.arange(128)
    cf[:, CF_HMA] = np.where(p < 64, NEG, 0.0)
    cf[:, CF_HMM] = np.where(p < 64, 0.0, 1.0)
    g = 1.0 - 2.0 ** (-5.0 - np.arange(8, dtype=np.float64))
    for h in range(8):
        cf[:, CF_TE + h] = g[h] ** (127 - p)
        cf[:, CF_FS + h * 128:CF_FS + (h + 1) * 128] = (g[h] ** (p + 1.0))[None, :]
        k = p[:, None]
        q = p[None, :]
        same = (k // 64) == (q // 64)
        m = np.where(same, g[h] ** np.abs(q - k), np.where(k < q, g[h] ** np.maximum(q - k, 0), 0.0))
        cf[:, CF_M + h * 128:CF_M + (h + 1) * 128] = m
    half = 64
    cf[:, CF_IF:CF_IF + 64] = (10000.0 ** (-np.arange(half, dtype=np.float32) / half)).astype(np.float32)[None, :]
    o1 = np.zeros((32, 383), np.float32)
    for m_ in range(383):
        o1[_t5_bucket(127 - m_), m_] = 1.0
    g128 = [float(g[h] ** 128) for h in range(8)]
    return cf, o1, g128


class Arena:
    def __init__(self, nc, nbytes):
        self.nc = nc
        t = nc.alloc_sbuf_tensor("arena", [128, nbytes // 4], F32)
        self.base = nc.lookup_mloc(t).addr
        self.size = nbytes
        self.off = 0
        self.n = 0

    def mark(self):
        return self.off

    def reset(self, m):
        self.off = m

    def t(self, name, shape, dt):
        nb = int(np.prod(shape[1:])) * (4 if dt in (F32, I32) else 2)
        nb = (nb + 63) // 64 * 64
        assert self.off + nb <= self.size, (name, self.off, nb, self.size)
        self.n += 1
        h = self.nc.alloc_sbuf_tensor_at("%s_%d" % (name, self.n), shape, dt, offset=self.base + self.off)
        self.off += nb
        return h


def build(dbg=False, stop_after=99, phases=None, vlist=None):
    nc = bass.Bass("TRN2", target_bir_lowering=False)
    es = ExitStack()
    P = Prog(nc, es)
    cf_np, o1_np, g128 = host_consts()

    def din(name, shape, dt=F32):
        return nc.dram_tensor(name, shape, dt, kind="ExternalInput")

    xv = din("xv", [NVIRT, D]).ap()
    p_o = din("p_o", [NOWN, 256]).ap()
    posv = din("posv", [NVIRT, 1], I32).ap()
    posT_d = din("posT", [128, 32], I32).ap()
    w_in = din("w_in", [D, C_END]).ap()
    gainT_d = din("gainT", [128, KC]).ap()
    w_a = din("w_a_out", [1024, D]).ap()
    w_b = din("w_b_out", [1024, D]).ap()
    w_o = din("w_o", [D, D]).ap()
    w_g = din("w_ple_gate", [D, D]).ap()
    w_p = din("w_ple", [256, D]).ap()
    gn_d = din("ret_gn_gain", [1, 1024]).ap()
    relb_d = din("rel_bias", [32, 8]).ap()
    relrow_d = din("rel_row", [1, 256]).ap()
    fg_d = din("final_gain", [1, D]).ap()
    cmask_d = din("cmask", [128, 2]).ap()
    cf_d = din("cf", [128, CF_N]).ap()
    o1_d = din("o1", [32, 383]).ap()
    out_d = nc.dram_tensor("out", [NOWN, D], F32, kind="ExternalOutput").ap()

    sk = "ExternalOutput" if dbg else "Internal"

    def scr(name, shape, dt):
        return nc.dram_tensor(name, shape, dt, kind=sk)

    S_akT = scr("S_akT", [8, 128, NVIRT], BF16).ap()
    S_aqT = scr("S_aqT", [8, 128, NOWN], BF16).ap()
    S_iqT = scr("S_iqT", [8, 128, NOWN], BF16).ap()
    S_av = scr("S_av", [NVIRT, 1024], BF16).ap()
    S_saz = scr("S_saz", [NOWN, 1024], BF16).ap()
    S_ikn = scr("S_ikn", [NVIRT, 64], BF16).ap()
    S_iw = scr("S_iw", [NOWN, 16], F32).ap()
    S_rq = scr("S_rq", [NOWN, 1024], F32).ap()
    S_rk = scr("S_rk", [NVIRT, 1024], F32).ap()
    S_rv = scr("S_rv", [NVIRT, 1024], BF16).ap()
    S_srz = scr("S_srz", [NOWN, 1024], BF16).ap()
    S_sga = scr("S_sga", [NOWN, D], BF16).ap()
    S_sgb = scr("S_sgb", [NOWN, D], BF16).ap()
    S_F_t = scr("S_F", [8, 383], F32)
    S_bo = scr("S_bo", [NOWN, 1024], BF16).ap()
    S_ao = scr("S_ao", [NOWN, 1024], BF16).ap()
    S_mT = scr("S_mT", [16, 128, 32 * 128], BF16).ap()
    S_mg = scr("S_mg", [NOWN, D], BF16).ap()
    S_r = scr("S_r", [NOWN, D], F32).ap()
    S_rb = scr("S_rb", [NOWN, D], BF16).ap()

    A = Arena(nc, 200 * 1024)
    banks = [nc.alloc_psum_tensor("bank%d" % i, [128, 512], F32) for i in range(8)]
    bankb = [b[:].bitcast(BF16) for b in banks]

    def dma(out, in_, key, r=(), w=(), pw=()):
        return P.add("sp", lambda e: e.dma_start(out=out, in_=in_), r=r, w=w, pw=pw, dma=key)

    def mm(out, lhsT, rhs, start, stop, r, w=(), pw=()):
        return P.add("pe", lambda e: e.matmul(out, lhsT=lhsT, rhs=rhs, start=start, stop=stop), r=r, w=w, pw=pw)

    def tr(out, in_, ident, r, w=(), pw=()):
        return P.add("pe", lambda e: e.transpose(out=out, in_=in_, identity=ident), r=r, w=w, pw=pw)

    def act(out, in_, func, r, w=(), pw=(), scale=1.0, bias=0.0):
        return P.add("act", lambda e: e.activation(out=out, in_=in_, func=func, scale=scale, bias=bias), r=r, w=w, pw=pw)

    def ts(eng, out, in0, s1, s2, op0, op1, r, w=(), pw=()):
        if op1 is None:
            return P.add(eng, lambda e: e.tensor_scalar(out=out, in0=in0, scalar1=s1, scalar2=None, op0=op0), r=r, w=w, pw=pw)
        return P.add(eng, lambda e: e.tensor_scalar(out=out, in0=in0, scalar1=s1, scalar2=s2, op0=op0, op1=op1), r=r, w=w, pw=pw)

    def tt(eng, out, in0, in1, op, r, w=(), pw=()):
        return P.add(eng, lambda e: e.tensor_tensor(out=out, in0=in0, in1=in1, op=op), r=r, w=w, pw=pw)

    def stt(out, in0, scalar, in1, op0, op1, r, w=(), pw=()):
        return P.add("dve", lambda e: e.scalar_tensor_tensor(out=out, in0=in0, scalar=scalar, in1=in1, op0=op0, op1=op1), r=r, w=w, pw=pw)

    def cp(eng, out, in_, r, w=(), pw=()):
        if eng == "act":
            return P.add("act", lambda e: e.activation(out=out, in_=in_, func=AF.Copy), r=r, w=w, pw=pw)
        return P.add(eng, lambda e: e.tensor_copy(out=out, in_=in_), r=r, w=w, pw=pw)

    def memset(eng, ap, val, w):
        return P.add(eng, lambda e: e.memset(ap, val), w=w)

    bank_rr = [0]

    def nextbank(lo=0, hi=8):
        b = lo + bank_rr[0] % (hi - lo)
        bank_rr[0] += 1
        return b

    ident_f = A.t("ident_f", [128, 128], F32)
    ident = A.t("ident", [128, 128], BF16)
    jf = A.t("jf", [128, 128], F32)
    cf = A.t("cf", [128, CF_N], F32)
    cmask = A.t("cmask", [128, 2], F32)
    gainT = A.t("gainT", [128, KC], F32)
    tab = A.t("tab", [128, 256], F32)
    negb15 = A.t("negb15", [128, 8], F32)
    E0 = A.t("E0", [128, 8, 128], F32)
    E1 = A.t("E1", [128, 8, 128], F32)
    m_const = A.mark()
    memset("pool", ident_f[:], 1.0, ["ident_f"])
    P.add("pool", lambda e: e.affine_select(out=ident_f[:], in_=ident_f[:], pattern=[[-1, 128]], compare_op=ALU.is_equal,
                                           fill=0.0, base=0, channel_multiplier=1), r=["ident_f"], w=["ident_f"])
    cp("dve", ident[:], ident_f[:], ["ident_f"], ["ident"])
    memset("pool", jf[:], 1.0, ["jf"])
    P.add("pool", lambda e: e.affine_select(out=jf[:], in_=jf[:], pattern=[[1, 128]], compare_op=ALU.is_equal,
                                           fill=0.0, base=-127, channel_multiplier=1), r=["jf"], w=["jf"])
    dma(cf[:], cf_d, "c1", w=["cf"])
    dma(cmask[:], cmask_d, "c2", w=["cmask"])
    dma(gainT[:], gainT_d, "c3", w=["gainT"])
    dma(tab[:], relrow_d.partition_broadcast(128), "c4", w=["tab"])
    ts("dve", negb15[:], tab[:, 120:128], -(128.0 ** 0.5), None, ALU.mult, None, ["tab"], ["negb15"])
    relb = A.t("relb", [32, 8], F32)
    o1 = A.t("o1", [32, 383], F32)
    Fsb = A.t("Fsb", [8, 383], F32)
    Hk = A.t("Hk", [128, 8, 128], F32)
    dma(relb[:], relb_d, "c5", w=["relb"])
    dma(o1[:], o1_d, "c6", w=["o1"])
    mm(banks[0][0:8, 0:383], relb[:], o1[:], True, True, ["relb", "o1"], w=["b0"])
    cp("dve", Fsb[:], banks[0][0:8, 0:383], ["b0"], ["Fsb"])
    dma(S_F_t.ap(), Fsb[:], "c7", r=["Fsb"], w=["S_F"])
    for c, Et, nm in ((0, E0, "E0"), (128, E1, "E1")):
        hank = bass.AP(S_F_t, c, [[1, 128], [383, 8], [1, 128]])
        dma(Hk[:], hank, "c8", r=["S_F"], w=["Hk"])
        for hb in range(2):
            mm(banks[1 + hb][:, :], jf[:], Hk[:, hb * 4:(hb + 1) * 4, :].rearrange("p a b -> p (a b)"), True, True,
               ["jf", "Hk"], w=["b%d" % (1 + hb)])
        for h in range(8):
            act(Et[:, h, :], banks[1 + h // 4][:, (h % 4) * 128:(h % 4 + 1) * 128], AF.Identity,
                ["b%d" % (1 + h // 4), "negb15"], pw=[nm], scale=128.0 ** 0.5, bias=negb15[:, h:h + 1])
    P.barrier()
    A.reset(m_const)

    def proj_pass(v0, blocks):
        m0 = A.mark()
        hT = A.t("hT", [128, KC, 2048], BF16)
        xin = [A.t("xin", [128, D], F32) for _ in range(3)]
        xn = [A.t("xn", [128, D], BF16) for _ in range(3)]
        mv = [A.t("mv", [128, 4], F32) for _ in range(3)]
        wst = [A.t("wst", [128, 8, 512], F32) for _ in range(2)]
        wb = [A.t("wb", [128, KC, 512], BF16) for _ in range(2)]
        ostf = [A.t("ostf", [128, 512], F32) for _ in range(4)]
        ostb = [A.t("ostb", [128, 512], BF16) for _ in range(4)]
        st2 = [A.t("st2", [128, 6], F32) for _ in range(2)]
        mv2 = [A.t("mv2", [128, 4], F32) for _ in range(2)]
        w_view = w_in.rearrange("(kc p) c -> p kc c", p=128)

        def load(j):
            c0, n = blocks[j][0], blocks[j][1]
            for hf in range(2):
                dma(wst[hf][:, :, 0:n], w_view[:, hf * 8:(hf + 1) * 8, c0:c0 + n], "wst%d" % hf, w=["wst%d" % hf])

        def cast(j):
            n = blocks[j][1]
            for hf in range(2):
                cp("pool", wb[j % 2][:, hf * 8:(hf + 1) * 8, 0:n], wst[hf][:, :, 0:n], ["wst%d" % hf], pw=["wb%d" % (j % 2)])

        load(0)
        cast(0)
        for t in range(16):
            b = t % 3
            dma(xin[b][:], xv[v0 + t * 128:v0 + (t + 1) * 128, :], "xin%d" % b, w=["xin%d" % b])
            P.add("act", (lambda b: lambda e: e.activation(out=xn[b][:], in_=xin[b][:], func=AF.Square, accum_out=mv[b][:, 2:3]))(b),
                  r=["xin%d" % b], w=["mvb%d" % b, "xn%d" % b])
            act(mv[b][:, 3:4], mv[b][:, 2:3], AF.Sqrt, ["mvb%d" % b], w=["mvc%d" % b], scale=1.0 / D, bias=EPS)
            P.add("dve", (lambda b: lambda e: e.reciprocal(out=mv[b][:, 3:4], in_=mv[b][:, 3:4]))(b), r=["mvc%d" % b], w=["mvc%d" % b])
            act(xn[b][:], xin[b][:], AF.Copy, ["xin%d" % b, "mvc%d" % b], w=["xn%d" % b], scale=mv[b][:, 3:4])
            for half in range(2):
                bk = 2 * (t % 2) + half
                for j in range(8):
                    kc = half * 8 + j
                    tr(bankb[bk][:, j * 128:(j + 1) * 128], xn[b][:, kc * 128:(kc + 1) * 128], ident[:],
                       ["xn%d" % b, "ident"], w=["b%d.%d" % (bk, j)])
                tt("dve", hT[:, half * 8:(half + 1) * 8, t * 128:(t + 1) * 128],
                   bankb[bk].rearrange("p (a b) -> p a b", a=8),
                   gainT[:, half * 8:(half + 1) * 8].unsqueeze(2).to_broadcast([128, 8, 128]), ALU.mult,
                   ["b%d.%d" % (bk, j) for j in range(8)] + ["gainT"], pw=["hT%d" % t])
        hT_all = ["hT%d" % t for t in range(16)]

        slot = [0]
        evrr = [0]

        def compute(j):
            c0, n, kind, dst = blocks[j]
            wbj = wb[j % 2]
            wres = "wb%d" % (j % 2)
            if kind == "fm":
                for sub in range(n // 128):
                    for tb in range(4):
                        bk = nextbank(4, 8)
                        for kc in range(KC):
                            mm(banks[bk][:, :], wbj[:, kc, sub * 128:(sub + 1) * 128], hT[:, kc, tb * 512:(tb + 1) * 512],
                               kc == 0, kc == KC - 1, [wres] + hT_all[tb * 4:tb * 4 + 4],
                               w=["b%d" % bk] if kc == 0 else (), pw=() if kc == 0 else ["b%d" % bk])
                        s = slot[0] % 4
                        slot[0] += 1
                        eng = "act" if evrr[0] % 2 == 0 else "dve"
                        evrr[0] += 1
                        cp(eng, ostb[s][:], banks[bk][:, :], ["b%d" % bk], w=["ostb%d" % s])
                        dap, idx0 = dst
                        dma(dap[idx0 + sub, :, v0 + tb * 512:v0 + (tb + 1) * 512] if dap is S_akT else
                            dap[idx0 + sub, :, tb * 512:(tb + 1) * 512], ostb[s][:], "ostb%d" % s, r=["ostb%d" % s])
                return
            for t in range(16):
                bk = nextbank(4, 8)
                for kc in range(KC):
                    mm(banks[bk][:, 0:n], hT[:, kc, t * 128:(t + 1) * 128], wbj[:, kc, 0:n], kc == 0, kc == KC - 1,
                       [wres, "hT%d" % t], w=["b%d" % bk] if kc == 0 else (), pw=() if kc == 0 else ["b%d" % bk])
                s = slot[0] % 4
                slot[0] += 1
                ps = banks[bk][:, 0:n]
                if kind == "ikw":
                    b2 = t % 2
                    P.add("dve", (lambda b2, bk: lambda e: e.bn_stats(out=st2[b2][:], in_=banks[bk][:, 0:64]))(b2, bk),
                          r=["b%d" % bk], w=["st2%d" % b2])
                    P.add("dve", (lambda b2: lambda e: e.bn_aggr(out=mv2[b2][:, 0:2], in_=st2[b2][:]))(b2),
                          r=["st2%d" % b2], w=["mv2%d" % b2])
                    act(mv2[b2][:, 2:3], mv2[b2][:, 1:2], AF.Sqrt, ["mv2%d" % b2], w=["mv2b%d" % b2], bias=EPS)
                    P.add("dve", (lambda b2: lambda e: e.reciprocal(out=mv2[b2][:, 2:3], in_=mv2[b2][:, 2:3]))(b2),
                          r=["mv2b%d" % b2], w=["mv2b%d" % b2])
                    ts("dve", ostb[s][:, 0:64], banks[bk][:, 0:64], mv2[b2][:, 0:1], mv2[b2][:, 2:3], ALU.subtract, ALU.mult,
                       ["b%d" % bk, "mv2%d" % b2, "mv2b%d" % b2], w=["ostb%d" % s])
                    dma(S_ikn[v0 + t * 128:v0 + (t + 1) * 128, :], ostb[s][:, 0:64], "ostb%d" % s, r=["ostb%d" % s])
                    if dst is not None:
                        ts("dve", ostf[s][:, 0:16], banks[bk][:, 64:80], 1.0 / 32.0, None, ALU.mult, None, ["b%d" % bk], w=["ostf%d" % s])
                        dma(S_iw[t * 128:(t + 1) * 128, :], ostf[s][:, 0:16], "ostf%d" % s, r=["ostf%d" % s])
                    continue
                dap, coff, isv = dst
                row0 = (v0 if isv else 0) + t * 128
                if kind == "f32":
                    eng = "act" if evrr[0] % 2 == 0 else "dve"
                    evrr[0] += 1
                    cp(eng, ostf[s][:, 0:n], ps, ["b%d" % bk], w=["ostf%d" % s])
                    dma(dap[row0:row0 + 128, coff:coff + n], ostf[s][:, 0:n], "ostf%d" % s, r=["ostf%d" % s])
                else:
                    if kind == "bf":
                        eng = "act" if evrr[0] % 2 == 0 else "dve"
                        evrr[0] += 1
                        cp(eng, ostb[s][:, 0:n], ps, ["b%d" % bk], w=["ostb%d" % s])
                    else:
                        act(ostb[s][:, 0:n], ps, AF.Silu if kind == "silu" else AF.Sigmoid, ["b%d" % bk], w=["ostb%d" % s])
                    dma(dap[row0:row0 + 128, coff:coff + n], ostb[s][:, 0:n], "ostb%d" % s, r=["ostb%d" % s])

        for j in range(len(blocks)):
            if j + 1 < len(blocks):
                load(j + 1)
            compute(j)
            if j + 1 < len(blocks):
                cast(j + 1)
        P.barrier()
        A.reset(m0)

    def tmblocks(c0, width, kind, dap, isv):
        return [(c0 + i * 512, 512, kind, (dap, i * 512, isv)) for i in range(width // 512)]

    ctx_blocks = ([(C_AK + i * 512, 512, "fm", (S_akT, i * 4)) for i in range(2)]
                  + tmblocks(C_AV, 1024, "bf", S_av, True)
                  + [(C_IK, 80, "ikw", None)]
                  + tmblocks(C_RK, 1024, "f32", S_rk, True)
                  + tmblocks(C_RV, 1024, "bf", S_rv, True))
    own_blocks = ([(C_AQ + i * 512, 512, "fm", (S_aqT, i * 4)) for i in range(2)]
                  + [(C_AK + i * 512, 512, "fm", (S_akT, i * 4)) for i in range(2)]
                  + tmblocks(C_AV, 1024, "bf", S_av, True)
                  + tmblocks(C_AZ, 1024, "silu", S_saz, False)
                  + [(C_IQ + i * 512, 512, "fm", (S_iqT, i * 4)) for i in range(2)]
                  + [(C_IK, 80, "ikw", True)]
                  + tmblocks(C_RQ, 1024, "f32", S_rq, False)
                  + tmblocks(C_RK, 1024, "f32", S_rk, True)
                  + tmblocks(C_RV, 1024, "bf", S_rv, True)
                  + tmblocks(C_RZ, 1024, "silu", S_srz, False)
                  + tmblocks(C_GA, 2048, "sigmoid", S_sga, False)
                  + tmblocks(C_GB, 2048, "sigmoid", S_sgb, False))
    if stop_after >= 1 and (phases is None or 1 in phases):
        proj_pass(0, ctx_blocks)
        proj_pass(2048, own_blocks)

    def retention():
        m0 = A.mark()
        gnb = A.t("gnb", [128, 1024], F32)
        dma(gnb[:], gn_d.partition_broadcast(128), "gnb", w=["gnb"])
        state = A.t("state", [128, 8, 128], F32)
        statebf = A.t("statebf", [128, 8, 128], BF16)
        memset("dve", state[:], 0.0, ["state"])
        memset("dve", statebf[:], 0.0, ["statebf"])
        NV = 32
        cs_all = A.t("cs_all", [128, NV, 64], F32)
        sn_all = A.t("sn_all", [128, NV, 64], F32)
        csk_all = A.t("csk_all", [128, NV, 64], F32)
        snk_all = A.t("snk_all", [128, NV, 64], F32)
        PI = math.pi
        m1 = A.mark()
        posTi = A.t("posTi", [128, NV], I32)
        posTf = A.t("posTf", [128, NV], F32)
        ang = A.t("ang", [128, NV, 64], F32)
        kfi = A.t("kfi", [128, NV * 64], I32)
        kf = A.t("kf", [128, NV * 64], F32)
        rr = A.t("rr", [128, NV * 64], F32)
        yy = A.t("yy", [128, NV * 64], F32)
        wm = A.t("wm", [128, NV * 64], F32)
        zz = A.t("zz", [128, NV * 64], F32)
        pacc = A.t("pacc", [128, NV * 64], F32)
        angf = ang[:].rearrange("p a b -> p (a b)")
        dma(posTi[:], posT_d, "posTi", w=["posTi"])
        cp("dve", posTf[:], posTi[:], ["posTi"], w=["posTf"])
        tt("dve", ang[:], cf[:, CF_IF:CF_IF + 64].unsqueeze(1).to_broadcast([128, NV, 64]),
           posTf[:].unsqueeze(2).to_broadcast([128, NV, 64]), ALU.mult, ["cf", "posTf"], w=["ang"])
        ts("dve", kf[:], angf, 1.0 / (2 * PI), None, ALU.mult, None, ["ang"], w=["kf"])
        cp("dve", kfi[:], kf[:], ["kf"], w=["kfi"])
        cp("dve", kf[:], kfi[:], ["kfi"], w=["kf"])
        stt(rr[:], kf[:], -2 * PI, angf, ALU.mult, ALU.add, ["kf", "ang"], w=["rr"])
        SC = [(-1.0) ** k / math.factorial(2 * k + 1) for k in range(8)]

        def psin(dst, dres):
            tt("dve", zz[:], yy[:], yy[:], ALU.mult, ["yy"], w=["zz"])
            ts("dve", pacc[:], zz[:], SC[7], None, ALU.mult, None, ["zz"], w=["pacc"])
            for k in (6, 5, 4, 3, 2, 1):
                stt(pacc[:], pacc[:], SC[k], zz[:], ALU.add, ALU.mult, ["pacc", "zz"], w=["pacc"])
            stt(dst[:].rearrange("p a b -> p (a b)"), pacc[:], 1.0, yy[:], ALU.add, ALU.mult, ["pacc", "yy"], w=[dres])

        ts("dve", yy[:], rr[:], -PI, PI, ALU.max, ALU.min, ["rr"], w=["yy"])
        psin(sn_all, "sn_all")
        ts("dve", yy[:], rr[:], PI / 2, None, ALU.add, None, ["rr", "sn_all"], w=["yy"])
        ts("dve", wm[:], yy[:], PI, -2 * PI, ALU.is_gt, ALU.mult, ["yy"], w=["wm"])
        tt("dve", yy[:], yy[:], wm[:], ALU.add, ["yy", "wm"], w=["yy"])
        ts("dve", yy[:], yy[:], -PI, PI, ALU.max, ALU.min, ["yy"], w=["yy"])
        psin(cs_all, "cs_all")
        sc = 128.0 ** -0.5
        ts("pool", csk_all[:], cs_all[:], sc, None, ALU.mult, None, ["cs_all"], w=["csk_all"])
        ts("pool", snk_all[:], sn_all[:], sc, None, ALU.mult, None, ["sn_all"], w=["snk_all"])
        P.barrier()
        A.reset(m1)
        rk32 = [A.t("rk32", [128, 1024], F32) for _ in range(3)]
        rq32 = [A.t("rq32", [128, 1024], F32) for _ in range(3)]
        rv16 = [A.t("rv16", [128, 1024], BF16) for _ in range(3)]
        srz = [A.t("srz", [128, 1024], BF16) for _ in range(3)]
        tmp4 = [[A.t("tmp4", [128, 8, 64], F32) for _ in range(4)] for _ in range(2)]
        rkr = [A.t("rkr", [128, 1024], BF16) for _ in range(2)]
        rkte = [A.t("rkte", [128, 1024], BF16) for _ in range(2)]
        rqr = [A.t("rqr", [128, 1024], BF16) for _ in range(2)]
        rkT = [A.t("rkT", [128, 8, 128], BF16) for _ in range(2)]
        rqT = [A.t("rqT", [128, 8, 128], BF16) for _ in range(2)]
        rqsT = [A.t("rqsT", [128, 8, 128], BF16) for _ in range(2)]
        Sm = [A.t("Sm", [128, 8, 128], BF16) for _ in range(2)]
        stg = [A.t("stg", [128, 8, 6], F32) for _ in range(2)]
        mvg = [A.t("mvg", [128, 8, 2], F32) for _ in range(2)]
        rstd = [A.t("rstd", [128, 8], F32) for _ in range(2)]
        y32 = [A.t("y32", [128, 1024], F32) for _ in range(2)]
        bo = [A.t("bo", [128, 1024], BF16) for _ in range(2)]

        def rope(x32, xres, cT, sT, cres, out, ores, b, meng="pool"):
            xv4 = x32[:].rearrange("p (h t d) -> p h t d", h=8, t=2)
            ov4 = out[:].rearrange("p (h t d) -> p h t d", h=8, t=2)
            x1, x2 = xv4[:, :, 0, :], xv4[:, :, 1, :]
            cb_ = cT.unsqueeze(1).to_broadcast([128, 8, 64])
            sb_ = sT.unsqueeze(1).to_broadcast([128, 8, 64])
            tn = ["tmp4%d_%d" % (b, k) for k in range(4)]
            tt(meng, tmp4[b][0][:], x1, cb_, ALU.mult, [xres] + cres, w=[tn[0]])
            tt(meng, tmp4[b][1][:], x2, sb_, ALU.mult, [xres] + cres, w=[tn[1]])
            tt("dve", ov4[:, :, 0, :], tmp4[b][0][:], tmp4[b][1][:], ALU.subtract, [tn[0], tn[1]], w=[ores])
            tt(meng, tmp4[b][2][:], x1, sb_, ALU.mult, [xres] + cres, w=[tn[2]])
            tt(meng, tmp4[b][3][:], x2, cb_, ALU.mult, [xres] + cres, w=[tn[3]])
            tt("dve", ov4[:, :, 1, :], tmp4[b][2][:], tmp4[b][3][:], ALU.add, [tn[2], tn[3]], pw=[ores])

        vl = list(vlist) if vlist is not None else list(range(32))

        def loads(v):
            own = v >= 16
            o = v - 16
            c = v % 3
            dma(rk32[c][:], S_rk[v * 128:(v + 1) * 128, :], "rk32%d" % c, w=["rk32%d" % c])
            dma(rv16[c][:], S_rv[v * 128:(v + 1) * 128, :], "rv16%d" % c, w=["rv16%d" % c])
            if own:
                dma(rq32[c][:], S_rq[o * 128:(o + 1) * 128, :], "rq32%d" % c, w=["rq32%d" % c])
                dma(srz[c][:], S_srz[o * 128:(o + 1) * 128, :], "srz%d" % c, w=["srz%d" % c])

        for n_ in range(min(2, len(vl))):
            loads(vl[n_])
        for idx, v in enumerate(vl):
            if idx + 2 < len(vl):
                loads(vl[idx + 2])
            own = v >= 16
            o = v - 16
            b = v % 2
            c = v % 3
            rope(rk32[c], "rk32%d" % c, csk_all[:, v, :], snk_all[:, v, :], [], rkr[b], "rkr%d" % b, b)
            tt("pool", rkte[b][:].rearrange("p (h d) -> p h d", h=8), rkr[b][:].rearrange("p (h d) -> p h d", h=8),
               cf[:, CF_TE:CF_TE + 8].unsqueeze(2).to_broadcast([128, 8, 128]), ALU.mult, ["rkr%d" % b], w=["rkte%d" % b])
            if own:
                rope(rq32[c], "rq32%d" % c, cs_all[:, v, :], sn_all[:, v, :], [], rqr[b], "rqr%d" % b, b, meng="dve")
                for h in range(8):
                    tr(bankb[0][:, h * 128:(h + 1) * 128], rkr[b][:, h * 128:(h + 1) * 128], ident[:], ["rkr%d" % b],
                       w=["b0"] if h == 0 else (), pw=() if h == 0 else ["b0"])
                for h in range(8):
                    tr(bankb[1][:, h * 128:(h + 1) * 128], rqr[b][:, h * 128:(h + 1) * 128], ident[:], ["rqr%d" % b],
                       w=["b1"] if h == 0 else (), pw=() if h == 0 else ["b1"])
                cp("dve", rkT[b][:].rearrange("p a b -> p (a b)"), bankb[0][:, :], ["b0"], w=["rkT%d" % b])
                cp("dve", rqT[b][:].rearrange("p a b -> p (a b)"), bankb[1][:, :], ["b1"], w=["rqT%d" % b])
                tt("dve", rqsT[b][:].rearrange("p a b -> p (a b)"), bankb[1][:, :], cf[:, CF_FS:CF_FS + 1024], ALU.mult,
                   ["b1"], w=["rqsT%d" % b])
                for h in range(8):
                    bk = 2 + h // 4
                    mm(banks[bk][:, (h % 4) * 128:(h % 4 + 1) * 128], rkT[b][:, h, :], rqT[b][:, h, :], True, True,
                       ["rkT%d" % b, "rqT%d" % b], w=["b%d" % bk] if h % 4 == 0 else (), pw=() if h % 4 == 0 else ["b%d" % bk])
                for hb in range(2):
                    tt("dve", Sm[b][:, hb * 4:(hb + 1) * 4, :].rearrange("p a b -> p (a b)"), banks[2 + hb][:, :],
                       cf[:, CF_M + hb * 512:CF_M + (hb + 1) * 512], ALU.mult, ["b%d" % (2 + hb)],
                       w=["Sm%d" % b] if hb == 0 else (), pw=() if hb == 0 else ["Sm%d" % b])
                for h in range(8):
                    bk = 4 + h // 4
                    oc = banks[bk][:, (h % 4) * 128:(h % 4 + 1) * 128]
                    mm(oc, Sm[b][:, h, :], rv16[c][:, h * 128:(h + 1) * 128], True, False, ["Sm%d" % b, "rv16%d" % c],
                       w=["b%d" % bk] if h % 4 == 0 else (), pw=() if h % 4 == 0 else ["b%d" % bk])
                    mm(oc, rqsT[b][:, h, :], statebf[:, h, :], False, True, ["rqsT%d" % b, "statebf"], pw=["b%d" % bk])
                for h in range(8):
                    bk = 4 + h // 4
                    oc = banks[bk][:, (h % 4) * 128:(h % 4 + 1) * 128]
                    P.add("dve", (lambda h, oc, b: lambda e: e.bn_stats(out=stg[b][:, h, :], in_=oc))(h, oc, b), r=["b%d" % bk],
                          w=["stg%d" % b] if h == 0 else (), pw=() if h == 0 else ["stg%d" % b])
                for h in range(8):
                    P.add("dve", (lambda h, b: lambda e: e.bn_aggr(out=mvg[b][:, h, :], in_=stg[b][:, h, :]))(h, b), r=["stg%d" % b],
                          w=["mvg%d" % b] if h == 0 else (), pw=() if h == 0 else ["mvg%d" % b])
                act(rstd[b][:], mvg[b][:, :, 1], AF.Sqrt, ["mvg%d" % b], w=["rstd%d" % b], bias=EPS)
                P.add("dve", (lambda b: lambda e: e.reciprocal(out=rstd[b][:], in_=rstd[b][:]))(b), r=["rstd%d" % b], w=["rstd%d" % b])
                for h in range(8):
                    bk = 4 + h // 4
                    oc = banks[bk][:, (h % 4) * 128:(h % 4 + 1) * 128]
                    ts("dve", y32[b][:, h * 128:(h + 1) * 128], oc, mvg[b][:, h, 0:1], rstd[b][:, h:h + 1], ALU.subtract, ALU.mult,
                       ["b%d" % bk, "mvg%d" % b, "rstd%d" % b], w=["y32%d" % b] if h == 0 else (), pw=() if h == 0 else ["y32%d" % b])
                tt("pool", y32[b][:], y32[b][:], gnb[:], ALU.mult, ["y32%d" % b, "gnb"], w=["y32%d" % b])
                tt("pool", bo[b][:], y32[b][:], srz[c][:], ALU.mult, ["y32%d" % b, "srz%d" % c], w=["bo%d" % b])
                dma(S_bo[o * 128:(o + 1) * 128, :], bo[b][:], "bo%d" % b, r=["bo%d" % b])
            for h in range(8):
                bk = 6 + h // 4
                mm(banks[bk][:, (h % 4) * 128:(h % 4 + 1) * 128], rkte[b][:, h * 128:(h + 1) * 128], rv16[c][:, h * 128:(h + 1) * 128],
                   True, True, ["rkte%d" % b, "rv16%d" % c], w=["b%d" % bk] if h % 4 == 0 else (), pw=() if h % 4 == 0 else ["b%d" % bk])
            for h in range(8):
                bk = 6 + h // 4
                stt(state[:, h, :], state[:, h, :], g128[h], banks[bk][:, (h % 4) * 128:(h % 4 + 1) * 128], ALU.mult, ALU.add,
                    ["state", "b%d" % bk], w=["state"])
            cp("act", statebf[:], state[:], ["state"], w=["statebf"])
        P.barrier()
        A.reset(m0)

    if stop_after >= 2 and (phases is None or 2 in phases):
        retention()

    def indexer():
        m0 = A.mark()
        kiT2 = A.t("kiT2", [128, NVIRT], BF16)
        ikl_all = A.t("ikl_all", [128, 32, 128], BF16)
        iksrc = S_ikn.rearrange("(v p) d -> p v d", p=128)
        for g4 in range(4):
            vs = slice(g4 * 8, (g4 + 1) * 8)
            dma(ikl_all[:, vs, 0:64], iksrc[:, vs, :], "ikA%d" % g4, pw=["ikl_all"])
            dma(ikl_all[:, vs, 64:128], iksrc[:, vs, :], "ikB%d" % g4, pw=["ikl_all"])
        for g4 in range(4):
            bk = g4 % 2
            for jj in range(8):
                tr(bankb[bk][:, jj * 128:(jj + 1) * 128], ikl_all[:, g4 * 8 + jj, :], ident[:], ["ikl_all", "ident"],
                   w=["b%d" % bk] if jj == 0 else (), pw=() if jj == 0 else ["b%d" % bk])
            cp("dve", kiT2[:, g4 * 1024:(g4 + 1) * 1024], bankb[bk][:, :], ["b%d" % bk], pw=["kiT2"])
        NB = 4
        acc = [A.t("acc", [128, NVIRT], F32) for _ in range(NB)]
        mx8 = [A.t("mx8", [128, 8], F32) for _ in range(NB)]
        bm = [A.t("bm", [128, 1], F32) for _ in range(NB)]
        bcnt = [A.t("bcnt", [128, 1], F32) for _ in range(NB)]
        bg = [A.t("bg", [128, 1], F32) for _ in range(NB)]
        mask = [A.t("mask", [128, NVIRT], BF16) for _ in range(NB)]
        maskT = [A.t("maskT", [128, NVIRT], BF16) for _ in range(2)]
        qiTz = [A.t("qiTz", [128, 16, 128], BF16) for _ in range(2)]
        iwt = [A.t("iwt", [128, 16], F32) for _ in range(2)]
        Dg = [A.t("Dg", [128, 16, 128], BF16) for _ in range(2)]
        tmp = [A.t("rtmp", [128, 512], BF16) for _ in range(6)]
        for q in range(2):
            memset("pool", qiTz[q][:], 0.0, ["qiTz%d" % q])
        trr = [0]
        RANGE = 64.0
        NIT = 30
        ACT_RELU_HEADS = 11

        tinfo = {}

        def sloads(i):
            q = i % 2
            src = S_iqT[:, :, i * 128:(i + 1) * 128].rearrange("a p q -> p a q")
            qz4 = qiTz[q][:].rearrange("p (a two) q -> p a two q", two=2)
            dma(qz4[0:64, :, 0, :], src[0:64, :, :], "qiTa%d" % q, w=["qiTz%d" % q])
            dma(qz4[64:128, :, 1, :], src[64:128, :, :], "qiTb%d" % q, pw=["qiTz%d" % q])
            dma(iwt[q][:], S_iw[i * 128:(i + 1) * 128, :], "iwt%d" % q, w=["iwt%d" % q])

        def score(i, b):
            N = 2048 + 128 * (i + 1)
            q = i % 2
            if i == 0:
                sloads(0)
            if i + 1 < 16:
                sloads(i + 1)
            tt("pool", Dg[q][:], ident[:].unsqueeze(1).to_broadcast([128, 16, 128]),
               iwt[q][:].unsqueeze(2).to_broadcast([128, 16, 128]), ALU.mult, ["iwt%d" % q, "ident"], w=["Dg%d" % q])
            nblk = (N + 511) // 512
            for blk in range(nblk):
                c0 = blk * 512
                wd = min(512, N - c0)
                isctx = c0 < 2048
                ares = "acc%d_%d" % (b, blk)
                ib = 2 + blk % 2
                slots = {}

                def smm(h):
                    bk = nextbank(4, 8)
                    mm(banks[bk][:, 0:wd], qiTz[q][:, h, :], kiT2[:, c0:c0 + wd], True, True, ["qiTz%d" % q, "kiT2"], w=["b%d" % bk])
                    sl = trr[0] % 6
                    trr[0] += 1
                    slots[h] = sl
                    if h < ACT_RELU_HEADS:
                        act(tmp[sl][:, 0:wd], banks[bk][:, 0:wd], AF.Relu, ["b%d" % bk], w=["rtmp%d" % sl])
                    else:
                        ts("dve", tmp[sl][:, 0:wd], banks[bk][:, 0:wd], 0.0, None, ALU.max, None, ["b%d" % bk], w=["rtmp%d" % sl])

                def dmm(h):
                    sl = slots[h]
                    mm(banks[ib][:, 0:wd], Dg[q][:, h, :], tmp[sl][:, 0:wd], h == 0, h == 15, ["Dg%d" % q, "rtmp%d" % sl],
                       w=["b%d" % ib] if h == 0 else (), pw=() if h == 0 else ["b%d" % ib])

                LA = 3
                for t in range(16 + LA):
                    if t < 16:
                        smm(t)
                    if t - LA >= 0:
                        dmm(t - LA)
                    yield
                if isctx:
                    act(acc[b][:, c0:c0 + wd], banks[ib][:, 0:wd], AF.Identity, ["b%d" % ib, "cmask"], w=[ares], bias=cmask[:, 0:1])
                else:
                    act(acc[b][:, c0:c0 + wd], banks[ib][:, 0:wd], AF.Identity, ["b%d" % ib], w=[ares])
            allacc = ["acc%d_%d" % (b, k) for k in range(nblk)]
            ts("dve", acc[b][:, N - 64:N], acc[b][:, N - 64:N], cf[:, CF_HMA:CF_HMA + 1], None, ALU.add, None, allacc + ["cf"],
               w=["accall%d" % b])
            tinfo[b] = (N, allacc)

        mtr = [0]

        def score_group(g):
            for k in range(2):
                yield from score(2 * g + k, 2 * (g % 2) + k)

        def bisect_group(g):
            bs = [2 * (g % 2), 2 * (g % 2) + 1]
            dve_set, act_set = [bs[0]], [bs[1]]
            for b in bs:
                N = tinfo[b][0]
                P.add("dve", (lambda b, N: lambda e: e.max(out=mx8[b][:], in_=acc[b][:, 0:N]))(b, N), r=["accall%d" % b], w=["mx8%d" % b])
                if b in dve_set:
                    ts("dve", bm[b][:], mx8[b][:, 0:1], -RANGE / 2, None, ALU.add, None, ["mx8%d" % b], w=["bm%d" % b])
                else:
                    ts("dve", bm[b][:], mx8[b][:, 0:1], -1.0, RANGE / 2, ALU.mult, ALU.add, ["mx8%d" % b], w=["bm%d" % b])
            wk = RANGE / 2
            for it in range(NIT):
                last = it == NIT - 1
                for b in dve_set:
                    N = tinfo[b][0]
                    P.add("dve", (lambda b, N: lambda e: e.tensor_scalar(out=mask[b][:, 0:N], in0=acc[b][:, 0:N], scalar1=bm[b][:, 0:1],
                                                                         scalar2=None, op0=ALU.is_ge, op1=ALU.add, accum_out=bcnt[b][:]))(b, N),
                          r=["accall%d" % b, "bm%d" % b], w=["bcnt%d" % b, "mask%d" % b])
                for b in act_set:
                    N = tinfo[b][0]
                    P.add("act", (lambda b, N: lambda e: e.activation(out=mask[b][:, 0:N], in_=acc[b][:, 0:N], func=AF.Sign, bias=bm[b][:, 0:1],
                                                                      scale=1.0, accum_out=bcnt[b][:]))(b, N),
                          r=["accall%d" % b, "bm%d" % b], w=["bcnt%d" % b, "mask%d" % b])
                yield
                for b in dve_set:
                    ts("dve", bg[b][:], bcnt[b][:], 255.5, 1.0 if last else 0.5, ALU.is_ge, ALU.subtract, ["bcnt%d" % b], w=["bg%d" % b])
                    stt(bm[b][:], bg[b][:], wk, bm[b][:], ALU.mult, ALU.add, ["bg%d" % b, "bm%d" % b], w=["bm%d" % b])
                for b in act_set:
                    N = tinfo[b][0]
                    act(bg[b][:], bcnt[b][:], AF.Sign, ["bcnt%d" % b], w=["bg%d" % b], bias=float(N - 511))
                    act(bm[b][:], bg[b][:], AF.Identity, ["bg%d" % b, "bm%d" % b], w=["bm%d" % b], scale=-wk / 2, bias=bm[b][:, 0:1])
                    if last:
                        act(bm[b][:], bm[b][:], AF.Identity, ["bm%d" % b], w=["bm%d" % b], bias=wk / 2)
                wk = wk / 2
                yield
            for b in act_set:
                ts("dve", bm[b][:], bm[b][:], -1.0, None, ALU.mult, None, ["bm%d" % b], w=["bm%d" % b])

        def mask_group(g):
            for k in range(2):
                b = 2 * (g % 2) + k
                i = 2 * g + k
                N, allacc = tinfo[b]
                ts("dve", mask[b][:, 0:N], acc[b][:, 0:N], bm[b][:, 0:1], None, ALU.is_ge, None, ["accall%d" % b, "bm%d" % b],
                   w=["mask%d" % b, "accfree%d" % b] + allacc)
                K = N // 128
                mq = mtr[0] % 2
                mtr[0] += 1
                for gg in range((K + 7) // 8):
                    kts = list(range(gg * 8, min(K, gg * 8 + 8)))
                    bk = gg % 2
                    for jj, kt in enumerate(kts):
                        tr(bankb[bk][:, jj * 128:(jj + 1) * 128], mask[b][:, kt * 128:(kt + 1) * 128], ident[:], ["mask%d" % b, "ident"],
                           w=["b%d" % bk] if jj == 0 else (), pw=() if jj == 0 else ["b%d" % bk])
                    ts("dve", maskT[mq][:, gg * 1024:gg * 1024 + len(kts) * 128], bankb[bk][:, 0:len(kts) * 128], -1.0, 2048.0, ALU.add, ALU.mult,
                       ["b%d" % bk], w=["maskT%d" % mq] if gg == 0 else (), pw=() if gg == 0 else ["maskT%d" % mq])
                dma(S_mT[i, :, 0:N], maskT[mq][:, 0:N], "maskT%d" % mq, r=["maskT%d" % mq])

        for _ in score_group(0):
            pass
        NG = 8
        for g in range(NG):
            bis = bisect_group(g)
            sc = score_group(g + 1) if g + 1 < NG else iter(())
            per = 6
            for _ in bis:
                for _k in range(per):
                    next(sc, None)
            for _ in sc:
                pass
            mask_group(g)
        P.barrier()
        A.reset(m0)

    def attention():
        m0 = A.mark()
        akT = A.t("akT", [128, 8, NVIRT], BF16)
        Vaug = A.t("Vaug", [128, 32, 8, 129], BF16)
        vst = [A.t("vst", [128, 1024], BF16) for _ in range(2)]
        for h in range(8):
            dma(akT[:, h, :], S_akT[h], "akT%d" % h, w=["akT%d" % h])
        akall = ["akT%d" % h for h in range(8)]
        memset("pool", Vaug[:, :, :, 128:129], 1.0, ["Vones"])
        for v in range(32):
            dma(Vaug[:, v, :, 0:128], S_av[v * 128:(v + 1) * 128, :].rearrange("p (h d) -> p h d", h=8), "vld%d" % (v % 8), pw=["Vaug"])
        aqT = [A.t("aqT", [128, 8, 128], BF16) for _ in range(2)]
        mT = [A.t("mT", [128, NVIRT], BF16) for _ in range(2)]
        saz = [A.t("saz", [128, 1024], BF16) for _ in range(2)]
        mE = [A.t("mE", [128, 8, 2, 128], BF16) for _ in range(2)]
        pt = [A.t("pt", [128, 512], BF16) for _ in range(4)]
        ao = [A.t("ao", [128, 1024], BF16) for _ in range(2)]
        rc = A.t("rc", [128, 8], F32)
        rot = [0]
        def qloads(i):
            N = (17 + i) * 128
            b = i % 2
            dma(aqT[b][:], S_aqT[:, :, i * 128:(i + 1) * 128].rearrange("a p q -> p a q"), "aqT%d" % b, w=["aqT%d" % b])
            dma(mT[b][:, 0:N], S_mT[i, :, 0:N], "mT%d" % b, w=["mT%d" % b])
            dma(saz[b][:], S_saz[i * 128:(i + 1) * 128, :], "saz%d" % b, w=["saz%d" % b])

        qloads(0)
        for i in range(16):
            K = 17 + i
            N = K * 128
            b = i % 2
            if i + 1 < 16:
                qloads(i + 1)
            kt0, kt1 = K - 1, K - 2
            tt("pool", mE[b][:, :, 0, :], E0[:], mT[b][:, kt0 * 128:(kt0 + 1) * 128].unsqueeze(1).to_broadcast([128, 8, 128]),
               ALU.add, ["mT%d" % b], w=["mE%d" % b])
            tt("pool", mE[b][:, :, 1, :], E1[:], mT[b][:, kt1 * 128:(kt1 + 1) * 128].unsqueeze(1).to_broadcast([128, 8, 128]),
               ALU.add, ["mT%d" % b], pw=["mE%d" % b])
            units = [(h, list(range(g * 4, min(K, g * 4 + 4)))) for h in range(8) for g in range((K + 3) // 4)]
            ubank, uslot = {}, {}

            def qk(u):
                h, grp = units[u]
                bk = nextbank(0, 6)
                ubank[u] = bk
                for j, kt in enumerate(grp):
                    oc = banks[bk][:, j * 128:(j + 1) * 128]
                    mm(oc, akT[:, h, kt * 128:(kt + 1) * 128], aqT[b][:, h, :], True, False,
                       ["akT%d" % h, "aqT%d" % b], w=["b%d" % bk] if j == 0 else (), pw=() if j == 0 else ["b%d" % bk])
                    if kt >= kt1:
                        mb, mres = mE[b][:, h, 0 if kt == kt0 else 1, :], "mE%d" % b
                    else:
                        mb, mres = mT[b][:, kt * 128:(kt + 1) * 128], "mT%d" % b
                    mm(oc, ident[:], mb, False, True, ["ident", mres], pw=["b%d" % bk])
                wd = len(grp) * 128
                sl = rot[0] % 4
                rot[0] += 1
                uslot[u] = sl
                act(pt[sl][:, 0:wd], banks[bk][:, 0:wd], AF.Exp, ["b%d" % bk, "tab"], w=["pt%d" % sl],
                    scale=128.0 ** -0.5, bias=tab[:, 120 + h:121 + h])

            def pvs(u):
                h, grp = units[u]
                pv = 6 + h % 2
                sl = uslot[u]
                for j, kt in enumerate(grp):
                    mm(banks[pv][:, 0:129], pt[sl][:, j * 128:(j + 1) * 128], Vaug[:, kt, h, :], kt == 0, kt == K - 1,
                       ["pt%d" % sl, "Vaug", "Vones"], w=["b%d" % pv] if kt == 0 else (), pw=() if kt == 0 else ["b%d" % pv])
                if grp[-1] == K - 1:
                    P.add("dve", (lambda h, pv: lambda e: e.reciprocal(out=rc[:, h:h + 1], in_=banks[pv][:, 128:129]))(h, pv),
                          r=["b%d" % pv], w=["rc%d" % h])
                    stt(ao[b][:, h * 128:(h + 1) * 128], banks[pv][:, 0:128], rc[:, h:h + 1], saz[b][:, h * 128:(h + 1) * 128],
                        ALU.mult, ALU.mult, ["b%d" % pv, "rc%d" % h, "saz%d" % b], w=["ao%d" % b] if h == 0 else (), pw=() if h == 0 else ["ao%d" % b])

            LA = 3
            for t in range(len(units) + LA):
                if t < len(units):
                    qk(t)
                if t - LA >= 0:
                    pvs(t - LA)
            dma(S_ao[i * 128:(i + 1) * 128, :], ao[b][:], "ao%d" % b, r=["ao%d" % b])
        P.barrier()
        A.reset(m0)

    def load_w(W_d, Wsb, kcn, wst4, nm):
        view = W_d.rearrange("(kc p) c -> p kc c", p=128)
        k = 0
        for cb in range(4):
            for k0 in range(0, kcn, 2):
                kk = min(2, kcn - k0)
                sl = k % 4
                k += 1
                dma(wst4[sl][:, 0:kk, :], view[:, k0:k0 + kk, cb * 512:(cb + 1) * 512], "wst4%d" % sl, w=["wst4%d" % sl])
                cp(("pool", "act", "dve")[k % 3], Wsb[:, k0:k0 + kk, cb * 512:(cb + 1) * 512], wst4[sl][:, 0:kk, :], ["wst4%d" % sl], pw=[nm])

    def transpose_tile(src, sres, kcn, dstT, dres, bk0):
        for g in range((kcn + 7) // 8):
            n = min(8, kcn - g * 8)
            bk = bk0 + g
            for j in range(n):
                kc = g * 8 + j
                tr(bankb[bk][:, j * 128:(j + 1) * 128], src[:, kc * 128:(kc + 1) * 128], ident[:], [sres, "ident"],
                   w=["b%d" % bk] if j == 0 else (), pw=() if j == 0 else ["b%d" % bk])
            cp("dve", dstT[:, g * 8:g * 8 + n, :].rearrange("p a b -> p (a b)"), bankb[bk][:, 0:n * 128], ["b%d" % bk],
               w=[dres] if g == 0 else (), pw=() if g == 0 else [dres])

    def merge():
        m0 = A.mark()
        Wa = A.t("Wa", [128, 8, D], BF16)
        Wb = A.t("Wb", [128, 8, D], BF16)
        wst4 = [A.t("wst4", [128, 2, 512], F32) for _ in range(4)]
        load_w(w_a, Wa, 8, wst4, "Wa")
        load_w(w_b, Wb, 8, wst4, "Wb")
        ao_t = [A.t("ao_t", [128, 1024], BF16) for _ in range(3)]
        bo_t = [A.t("bo_t", [128, 1024], BF16) for _ in range(3)]
        sga_t = [A.t("sga_t", [128, D], BF16) for _ in range(3)]
        sgb_t = [A.t("sgb_t", [128, D], BF16) for _ in range(3)]
        aoT = [A.t("aoT", [128, 8, 128], BF16) for _ in range(2)]
        boT = [A.t("boT", [128, 8, 128], BF16) for _ in range(2)]
        m1 = [A.t("m1", [128, 512], F32) for _ in range(2)]
        m2 = [A.t("m2", [128, 512], F32) for _ in range(2)]
        mg = [A.t("mg", [128, D], BF16) for _ in range(2)]
        kctr = [0]

        def stage0(t):
                c = t % 3
                rows = slice(t * 128, (t + 1) * 128)
                dma(ao_t[c][:], S_ao[rows, :], "ao_t%d" % c, w=["ao_t%d" % c])
                dma(bo_t[c][:], S_bo[rows, :], "bo_t%d" % c, w=["bo_t%d" % c])
                dma(sga_t[c][:], S_sga[rows, :], "sga_t%d" % c, w=["sga_t%d" % c])
                dma(sgb_t[c][:], S_sgb[rows, :], "sgb_t%d" % c, w=["sgb_t%d" % c])

        def stage1(t):
                b = t % 2
                c = t % 3
                transpose_tile(ao_t[c], "ao_t%d" % c, 8, aoT[b], "aoT%d" % b, 0)
                transpose_tile(bo_t[c], "bo_t%d" % c, 8, boT[b], "boT%d" % b, 1)

        def stage2(t):
                b = t % 2
                rows = slice(t * 128, (t + 1) * 128)
                for cb in range(4):
                    cols = slice(cb * 512, (cb + 1) * 512)
                    bkA = nextbank(2, 8)
                    for kc in range(8):
                        mm(banks[bkA][:, :], aoT[b][:, kc, :], Wa[:, kc, cols], kc == 0, kc == 7, ["aoT%d" % b, "Wa"],
                           w=["b%d" % bkA] if kc == 0 else (), pw=() if kc == 0 else ["b%d" % bkA])
                    bkB = nextbank(2, 8)
                    for kc in range(8):
                        mm(banks[bkB][:, :], boT[b][:, kc, :], Wb[:, kc, cols], kc == 0, kc == 7, ["boT%d" % b, "Wb"],
                           w=["b%d" % bkB] if kc == 0 else (), pw=() if kc == 0 else ["b%d" % bkB])
                    sl = kctr[0] % 2
                    kctr[0] += 1
                    tt("dve", m1[sl][:], banks[bkA][:, :], sga_t[t % 3][:, cols], ALU.mult, ["b%d" % bkA, "sga_t%d" % (t % 3)], w=["m1%d" % sl])
                    tt("dve", m2[sl][:], banks[bkB][:, :], sgb_t[t % 3][:, cols], ALU.mult, ["b%d" % bkB, "sgb_t%d" % (t % 3)], w=["m2%d" % sl])
                    tt("pool", mg[b][:, cols], m1[sl][:], m2[sl][:], ALU.add, ["m1%d" % sl, "m2%d" % sl],
                       w=["mg%d" % b] if cb == 0 else (), pw=() if cb == 0 else ["mg%d" % b])
                dma(S_mg[rows, :], mg[b][:], "mg%d" % b, r=["mg%d" % b])

        stage0(0)
        stage0(1)
        stage1(0)
        for t in range(16):
            if t + 2 < 16:
                stage0(t + 2)
            if t + 1 < 16:
                stage1(t + 1)
            stage2(t)
        P.barrier()
        A.reset(m0)

    def outproj():
        m0 = A.mark()
        Wo = A.t("Wo", [128, 16, D], BF16)
        wst4 = [A.t("wst4", [128, 2, 512], F32) for _ in range(4)]
        load_w(w_o, Wo, 16, wst4, "Wo")
        mg_t = [A.t("mg_t", [128, D], BF16) for _ in range(3)]
        x_t = [A.t("x_t", [128, D], F32) for _ in range(3)]
        mgT = [A.t("mgT", [128, 16, 128], BF16) for _ in range(2)]
        r_t = [A.t("r_t", [128, D], F32) for _ in range(2)]
        rb_t = [A.t("rb_t", [128, D], BF16) for _ in range(2)]
        def stage0(t):
                c = t % 3
                rows = slice(t * 128, (t + 1) * 128)
                dma(mg_t[c][:], S_mg[rows, :], "mg_t%d" % c, w=["mg_t%d" % c])
                dma(x_t[c][:], xv[2048 + t * 128:2048 + (t + 1) * 128, :], "x_t%d" % c, w=["x_t%d" % c])

        def stage1(t):
                b = t % 2
                c = t % 3
                transpose_tile(mg_t[c], "mg_t%d" % c, 16, mgT[b], "mgT%d" % b, 0)

        def stage2(t):
                b = t % 2
                rows = slice(t * 128, (t + 1) * 128)
                for cb in range(4):
                    cols = slice(cb * 512, (cb + 1) * 512)
                    bk = nextbank(2, 8)
                    for kc in range(16):
                        mm(banks[bk][:, :], mgT[b][:, kc, :], Wo[:, kc, cols], kc == 0, kc == 15, ["mgT%d" % b, "Wo"],
                           w=["b%d" % bk] if kc == 0 else (), pw=() if kc == 0 else ["b%d" % bk])
                    tt("dve", r_t[b][:, cols], banks[bk][:, :], x_t[t % 3][:, cols], ALU.add, ["b%d" % bk, "x_t%d" % (t % 3)],
                       w=["r_t%d" % b] if cb == 0 else (), pw=() if cb == 0 else ["r_t%d" % b])
                cp("pool", rb_t[b][:], r_t[b][:], ["r_t%d" % b], w=["rb_t%d" % b])
                dma(S_r[rows, :], r_t[b][:], "r_t%d" % b, r=["r_t%d" % b])
                dma(S_rb[rows, :], rb_t[b][:], "rb_t%d" % b, r=["rb_t%d" % b])

        stage0(0)
        stage0(1)
        stage1(0)
        for t in range(16):
            if t + 2 < 16:
                stage0(t + 2)
            if t + 1 < 16:
                stage1(t + 1)
            stage2(t)
        P.barrier()
        A.reset(m0)

    def final():
        m0 = A.mark()
        Wg = A.t("Wg", [128, 16, D], BF16)
        Wp = A.t("Wp", [128, 2, D], BF16)
        fgb = A.t("fgb", [128, D], F32)
        wst4 = [A.t("wst4", [128, 2, 512], F32) for _ in range(4)]
        load_w(w_g, Wg, 16, wst4, "Wg")
        load_w(w_p, Wp, 2, wst4, "Wp")
        dma(fgb[:], fg_d.partition_broadcast(128), "fgb", w=["fgb"])
        rb_t = [A.t("rb_t", [128, D], BF16) for _ in range(3)]
        r_t = [A.t("r_t", [128, D], F32) for _ in range(3)]
        p_t = [A.t("p_t", [128, 256], F32) for _ in range(3)]
        pb_t = [A.t("pb_t", [128, 256], BF16) for _ in range(3)]
        rT = [A.t("rT", [128, 16, 128], BF16) for _ in range(2)]
        pT = [A.t("pT", [128, 2, 128], BF16) for _ in range(2)]
        o_t = [A.t("o_t", [128, D], F32) for _ in range(2)]
        sg = [A.t("sg", [128, 512], F32) for _ in range(2)]
        t2 = [A.t("t2", [128, 512], F32) for _ in range(2)]
        st = [A.t("stf", [128, 4, 6], F32) for _ in range(2)]
        mv = [A.t("mvf", [128, 4], F32) for _ in range(2)]
        kctr = [0]

        def stage0(t):
                c = t % 3
                rows = slice(t * 128, (t + 1) * 128)
                dma(rb_t[c][:], S_rb[rows, :], "rb_t%d" % c, w=["rb_t%d" % c])
                dma(r_t[c][:], S_r[rows, :], "r_t%d" % c, w=["r_t%d" % c])
                dma(p_t[c][:], p_o[rows, :], "p_t%d" % c, w=["p_t%d" % c])
                cp("pool", pb_t[c][:], p_t[c][:], ["p_t%d" % c], w=["pb_t%d" % c])

        def stage1(t):
                b = t % 2
                c = t % 3
                transpose_tile(rb_t[c], "rb_t%d" % c, 16, rT[b], "rT%d" % b, 0)
                transpose_tile(pb_t[c], "pb_t%d" % c, 2, pT[b], "pT%d" % b, 2)

        def stage2(t):
                b = t % 2
                rows = slice(t * 128, (t + 1) * 128)
                for cb in range(4):
                    cols = slice(cb * 512, (cb + 1) * 512)
                    bkG = nextbank(3, 8)
                    for kc in range(16):
                        mm(banks[bkG][:, :], rT[b][:, kc, :], Wg[:, kc, cols], kc == 0, kc == 15, ["rT%d" % b, "Wg"],
                           w=["b%d" % bkG] if kc == 0 else (), pw=() if kc == 0 else ["b%d" % bkG])
                    bkP = nextbank(3, 8)
                    for kc in range(2):
                        mm(banks[bkP][:, :], pT[b][:, kc, :], Wp[:, kc, cols], kc == 0, kc == 1, ["pT%d" % b, "Wp"],
                           w=["b%d" % bkP] if kc == 0 else (), pw=() if kc == 0 else ["b%d" % bkP])
                    sl = kctr[0] % 2
                    kctr[0] += 1
                    act(sg[sl][:], banks[bkG][:, :], AF.Sigmoid, ["b%d" % bkG], w=["sg%d" % sl])
                    tt("dve", t2[sl][:], banks[bkP][:, :], sg[sl][:], ALU.mult, ["b%d" % bkP, "sg%d" % sl], w=["t2%d" % sl])
                    tt("pool", o_t[b][:, cols], t2[sl][:], r_t[t % 3][:, cols], ALU.add, ["t2%d" % sl, "r_t%d" % (t % 3)],
                       w=["o_t%d" % b] if cb == 0 else (), pw=() if cb == 0 else ["o_t%d" % b])
                for c in range(4):
                    P.add("dve", (lambda b, c: lambda e: e.bn_stats(out=st[b][:, c, :], in_=o_t[b][:, c * 512:(c + 1) * 512]))(b, c),
                          r=["o_t%d" % b], w=["stf%d" % b] if c == 0 else (), pw=() if c == 0 else ["stf%d" % b])
                P.add("dve", (lambda b: lambda e: e.bn_aggr(out=mv[b][:, 0:2], in_=st[b][:].rearrange("p a b -> p (a b)")))(b),
                      r=["stf%d" % b], w=["mvf%d" % b])
                stt(mv[b][:, 2:3], mv[b][:, 0:1], mv[b][:, 0:1], mv[b][:, 1:2], ALU.mult, ALU.add, ["mvf%d" % b], w=["mvfb%d" % b])
                act(mv[b][:, 3:4], mv[b][:, 2:3], AF.Sqrt, ["mvfb%d" % b], w=["mvfc%d" % b], bias=EPS)
                P.add("dve", (lambda b: lambda e: e.reciprocal(out=mv[b][:, 3:4], in_=mv[b][:, 3:4]))(b), r=["mvfc%d" % b], w=["mvfc%d" % b])
                stt(o_t[b][:], o_t[b][:], mv[b][:, 3:4], fgb[:], ALU.mult, ALU.mult, ["o_t%d" % b, "mvfc%d" % b, "fgb"], w=["o_t%d" % b])
                dma(out_d[rows, :], o_t[b][:], "o_t%d" % b, r=["o_t%d" % b])

        stage0(0)
        stage0(1)
        stage1(0)
        for t in range(16):
            if t + 2 < 16:
                stage0(t + 2)
            if t + 1 < 16:
                stage1(t + 1)
            stage2(t)
        P.barrier()
        A.reset(m0)

    if stop_after >= 3 and (phases is None or 3 in phases):
        indexer()
    if stop_after >= 4 and (phases is None or 4 in phases):
        attention()
    if stop_after >= 5 and (phases is None or 5 in phases):
        merge()
    if stop_after >= 6 and (phases is None or 6 in phases):
        outproj()
    if stop_after >= 7 and (phases is None or 7 in phases):
        final()
    return nc, es, P, locals()


_CACHE = {}


def kernel(x, p, positions, w_in, norm_gain, w_a_out, w_b_out, w_o, ret_gn_gain, w_ple, w_ple_gate, rel_bias, final_gain):
    if "nc" not in _CACHE:
        nc, es, P, _ = build()
        finish(nc, es, P)
        _CACHE["nc"] = nc
    nc = _CACHE["nc"]
    maps = make_in_maps(x, p, positions, w_in, norm_gain, w_a_out, w_b_out, w_o, ret_gn_gain, w_ple, w_ple_gate,
                        rel_bias, final_gain)
    res = run_bass_kernel_spmd(nc, maps, core_ids=list(range(8)))
    out = np.zeros((4, 4096, D), np.float32)
    for c in range(8):
        b, half = c // 2, c % 2
        out[b, half * 2048:(half + 1) * 2048] = np.asarray(res.results[c]["out"], dtype=np.float32)
    return out


def finish(nc, es, P):
    i = P.add("sp", None)
    P.ops[i]["deps"] = set(P.last_dma.values()) | set(P.last_eng.values())
    P.emit()
    es.close()
    return nc


def make_in_maps(x, p, positions, w_in, norm_gain, w_a_out, w_b_out, w_o, ret_gn_gain, w_ple, w_ple_gate,
                 rel_bias, final_gain, cores=range(8)):
    cf_np, o1_np, _ = host_consts()
    f = lambda a: np.ascontiguousarray(a, dtype=np.float32)
    shared = {
        "w_in": f(w_in[0]), "gainT": f(np.asarray(norm_gain[0]).reshape(KC, 128).T),
        "w_a_out": f(w_a_out[0]), "w_b_out": f(w_b_out[0]), "w_o": f(w_o[0]), "w_ple_gate": f(w_ple_gate[0]),
        "w_ple": f(w_ple[0]), "ret_gn_gain": f(np.asarray(ret_gn_gain[0]).reshape(1, 1024)),
        "rel_bias": f(rel_bias), "rel_row": f(np.asarray(rel_bias).reshape(1, 256)),
        "final_gain": f(np.asarray(final_gain).reshape(1, D)), "cf": cf_np, "o1": o1_np,
    }
    maps = []
    for c in cores:
        b, half = c // 2, c % 2
        xb = np.asarray(x[b])
        pb = np.asarray(positions[b]).astype(np.int32)
        if half == 1:
            xvv = f(xb)
            pv = pb.reshape(NVIRT, 1)
            cm = np.tile(np.array([[0.0, 1.0]], np.float32), (128, 1))
        else:
            xvv = np.zeros((NVIRT, D), np.float32)
            xvv[2048:] = xb[:2048]
            pv = np.zeros((NVIRT, 1), np.int32)
            pv[2048:, 0] = pb[:2048]
            cm = np.tile(np.array([[NEG, 0.0]], np.float32), (128, 1))
        m = dict(shared)
        m.update({"xv": xvv, "posv": np.ascontiguousarray(pv), "cmask": cm,
                  "posT": np.ascontiguousarray(pv.reshape(32, 128).T),
                  "p_o": f(np.asarray(p[0, b, half * 2048:(half + 1) * 2048]))})
        maps.append(m)
    return maps
```
